# Optimizing a Trainium2 kernel written in Bass

```python
import math
import jax, jax.numpy as jnp
from jax import lax
import numpy as np

D_MODEL = 2048
BATCH = 8
SEQ = 2048
DEPTH = 1

D_MIX = D_MODEL
CONV_WIDTH = 4
LRU_WIDTH = D_MIX // 2
LRU_HEADS = 16
LRU_HEAD_DIM = LRU_WIDTH // LRU_HEADS
LRU_C = 8.0
SSD_WIDTH = D_MIX - LRU_WIDTH
SSD_HEAD_DIM = 64
SSD_HEADS = SSD_WIDTH // SSD_HEAD_DIM
SSD_GROUPS = 2
SSD_STATE = 128
SSD_CHUNK = 128
SSD_CONV_DIM = SSD_WIDTH + 2 * SSD_GROUPS * SSD_STATE
IN_SPLITS = (LRU_WIDTH, LRU_WIDTH, SSD_WIDTH, SSD_CONV_DIM, SSD_HEADS)
IN_PROJ_DIM = sum(IN_SPLITS)
PEER_HEADS = 8
PEER_N_KEYS = 128
PEER_N_EXPERTS = PEER_N_KEYS * PEER_N_KEYS
PEER_QUERY_DIM = 256
PEER_HALF = PEER_QUERY_DIM // 2
PEER_TOPK = 16
PEER_TOKEN_BLOCK = 128
EPS = 1e-6

kernel_name = "hybrid_rglru_ssd_peer"


def rmsnorm(x, w):
    xf = x.astype(jnp.float32)
    y = xf * lax.rsqrt(jnp.mean(xf * xf, axis=-1, keepdims=True) + EPS)
    return (y * w.astype(jnp.float32)).astype(x.dtype)


def causal_depthwise_conv(x, w, b):
    c = x.shape[-1]
    y = lax.conv_general_dilated(
        x, w[:, None, :].astype(x.dtype), window_strides=(1,),
        padding=[(CONV_WIDTH - 1, 0)], dimension_numbers=('NWC', 'WIO', 'NWC'),
        feature_group_count=c)
    return y + b.astype(x.dtype)


def rglru(xb, wa, ba, wx, bx, lam):
    bsz, s, w = xb.shape
    xh = xb.reshape(bsz, s, LRU_HEADS, LRU_HEAD_DIM)
    r = jax.nn.sigmoid((jnp.einsum('bshi,hij->bshj', xh, wa).reshape(bsz, s, w) + ba).astype(jnp.float32))
    i = jax.nn.sigmoid((jnp.einsum('bshi,hij->bshj', xh, wx).reshape(bsz, s, w) + bx).astype(jnp.float32))
    log_a = -LRU_C * r * jax.nn.softplus(-lam.astype(jnp.float32))
    a = jnp.exp(log_a)
    b = jnp.sqrt(-jnp.expm1(2.0 * log_a)) * (i * xb.astype(jnp.float32))

    def step(h, ab):
        a_t, b_t = ab
        h = a_t * h + b_t
        return h, h

    _, hs = lax.scan(step, jnp.zeros((bsz, w), jnp.float32),
                     (a.transpose(1, 0, 2), b.transpose(1, 0, 2)))
    return hs.transpose(1, 0, 2).astype(xb.dtype)


def segsum(x):
    t = x.shape[-1]
    xx = jnp.broadcast_to(x[..., None], x.shape + (t,))
    xx = jnp.where(jnp.tril(jnp.ones((t, t), bool), -1), xx, 0.0)
    ss = jnp.cumsum(xx, axis=-2)
    return jnp.where(jnp.tril(jnp.ones((t, t), bool), 0), ss, -jnp.inf)


def ssd_chunked(x, dt, a, bm, cm):
    bsz, s, h, p = x.shape
    nc = s // SSD_CHUNK
    rep = h // SSD_GROUPS
    bh = jnp.repeat(bm, rep, axis=2).reshape(bsz, nc, SSD_CHUNK, h, SSD_STATE)
    ch = jnp.repeat(cm, rep, axis=2).reshape(bsz, nc, SSD_CHUNK, h, SSD_STATE)
    xc = (x * dt[..., None]).reshape(bsz, nc, SSD_CHUNK, h, p)
    adt = (a * dt).reshape(bsz, nc, SSD_CHUNK, h).transpose(0, 3, 1, 2)
    a_cs = jnp.cumsum(adt, axis=-1)
    lmat = jnp.exp(segsum(adt))
    scores = jnp.einsum('bclhn,bcshn->bhcls', ch, bh) * lmat
    y_diag = jnp.einsum('bhcls,bcshp->bclhp', scores, xc)
    decay_states = jnp.exp(a_cs[..., -1:] - a_cs)
    states = jnp.einsum('bclhn,bhcl,bclhp->bchpn', bh, decay_states, xc)
    states = jnp.concatenate([jnp.zeros_like(states[:, :1]), states], axis=1)
    chunk_decay = jnp.exp(segsum(jnp.pad(a_cs[..., -1], ((0, 0), (0, 0), (1, 0)))))
    states = jnp.einsum('bhzc,bchpn->bzhpn', chunk_decay, states)[:, :-1]
    y_off = jnp.einsum('bclhn,bchpn,bhcl->bclhp', ch, states, jnp.exp(a_cs))
    return (y_diag + y_off).reshape(bsz, s, h, p)


def peer(h, wq, sub_keys, u, v):
    bsz, s, d = h.shape
    q = (h @ wq).astype(jnp.float32).reshape(bsz, s, PEER_HEADS, 2, PEER_HALF)
    sc = jnp.einsum('bshkd,hknd->bshkn', q, sub_keys.astype(jnp.float32))
    s1, i1 = lax.top_k(sc[..., 0, :], PEER_TOPK)
    s2, i2 = lax.top_k(sc[..., 1, :], PEER_TOPK)
    cand_s = (s1[..., :, None] + s2[..., None, :]).reshape(bsz, s, PEER_HEADS, PEER_TOPK * PEER_TOPK)
    cand_i = (i1[..., :, None] * PEER_N_KEYS + i2[..., None, :]).reshape(bsz, s, PEER_HEADS, PEER_TOPK * PEER_TOPK)
    top_s, top_pos = lax.top_k(cand_s, PEER_TOPK)
    idx = jnp.take_along_axis(cand_i, top_pos, axis=-1)
    g = jax.nn.softmax(top_s, axis=-1)
    n_blocks = (bsz * s) // PEER_TOKEN_BLOCK
    hb = h.reshape(n_blocks, PEER_TOKEN_BLOCK, d)
    ib = idx.reshape(n_blocks, PEER_TOKEN_BLOCK, PEER_HEADS * PEER_TOPK)
    gb = g.astype(h.dtype).reshape(n_blocks, PEER_TOKEN_BLOCK, PEER_HEADS * PEER_TOPK)

    def block(args):
        hx, ix, gx = args
        u_sel = jnp.take(u, ix, axis=0)
        act = jax.nn.gelu(jnp.einsum('tkd,td->tk', u_sel, hx))
        v_sel = jnp.take(v, ix, axis=0)
        return jnp.einsum('tk,tkd->td', gx * act, v_sel)

    out = lax.map(block, (hb, ib, gb))
    return out.reshape(bsz, s, d)


def setup_inputs(seed: int = 0) -> dict:
    key = jax.random.key(seed)
    ks = jax.random.split(key, 32)
    f32 = jnp.float32

    def nrm(k, shape, scale):
        return jax.random.normal(k, shape, f32) * scale

    def gain(k, shape):
        return 1.0 + 0.02 * jax.random.normal(k, shape, f32)

    a8 = jax.random.uniform(ks[10], (DEPTH, LRU_WIDTH), f32, 0.9, 0.999)
    sig = a8 ** (1.0 / LRU_C)
    lru_lambda = jnp.log(sig) - jnp.log1p(-sig)
    dt0 = jnp.exp(jax.random.uniform(ks[13], (DEPTH, SSD_HEADS), f32, math.log(1e-3), math.log(1e-1)))
    ssd_dt_bias = dt0 + jnp.log(-jnp.expm1(-dt0))
    ssd_a_log = jnp.log(jax.random.uniform(ks[14], (DEPTH, SSD_HEADS), f32, 1.0, 16.0))
    return {
        'x': nrm(ks[0], (BATCH, SEQ, D_MODEL), 1.0),
        'norm_mix_w': gain(ks[1], (DEPTH, D_MODEL)),
        'w_in': nrm(ks[2], (DEPTH, D_MODEL, IN_PROJ_DIM), D_MODEL ** -0.5),
        'lru_conv_w': nrm(ks[3], (DEPTH, CONV_WIDTH, LRU_WIDTH), CONV_WIDTH ** -0.5),
        'lru_conv_b': nrm(ks[4], (DEPTH, LRU_WIDTH), 0.02),
        'lru_wa': nrm(ks[5], (DEPTH, LRU_HEADS, LRU_HEAD_DIM, LRU_HEAD_DIM), LRU_HEAD_DIM ** -0.5),
        'lru_ba': nrm(ks[6], (DEPTH, LRU_WIDTH), 0.02),
        'lru_wx': nrm(ks[7], (DEPTH, LRU_HEADS, LRU_HEAD_DIM, LRU_HEAD_DIM), LRU_HEAD_DIM ** -0.5),
        'lru_bx': nrm(ks[8], (DEPTH, LRU_WIDTH), 0.02),
        'lru_lambda': lru_lambda,
        'ssd_conv_w': nrm(ks[11], (DEPTH, CONV_WIDTH, SSD_CONV_DIM), CONV_WIDTH ** -0.5),
        'ssd_conv_b': nrm(ks[12], (DEPTH, SSD_CONV_DIM), 0.02),
        'ssd_dt_bias': ssd_dt_bias,
        'ssd_a_log': ssd_a_log,
        'ssd_d': gain(ks[15], (DEPTH, SSD_HEADS)),
        'ssd_norm_w': gain(ks[16], (DEPTH, SSD_WIDTH)),
        'w_out': nrm(ks[17], (DEPTH, D_MIX, D_MODEL), D_MIX ** -0.5),
        'norm_ffn_w': gain(ks[18], (DEPTH, D_MODEL)),
        'peer_wq': nrm(ks[19], (DEPTH, D_MODEL, PEER_HEADS * PEER_QUERY_DIM), D_MODEL ** -0.5),
        'peer_sub_keys': nrm(ks[20], (DEPTH, PEER_HEADS, 2, PEER_N_KEYS, PEER_HALF), PEER_HALF ** -0.5),
        'peer_u': nrm(ks[21], (DEPTH, PEER_N_EXPERTS, D_MODEL), D_MODEL ** -0.5),
        'peer_v': nrm(ks[22], (DEPTH, PEER_N_EXPERTS, D_MODEL), 0.1),
        'norm_final_w': gain(ks[23], (D_MODEL,)),
    }


def reference(x, norm_mix_w, w_in, lru_conv_w, lru_conv_b, lru_wa, lru_ba, lru_wx, lru_bx,
              lru_lambda, ssd_conv_w, ssd_conv_b, ssd_dt_bias, ssd_a_log, ssd_d, ssd_norm_w,
              w_out, norm_ffn_w, peer_wq, peer_sub_keys, peer_u, peer_v, norm_final_w):
    bsz, s, _ = x.shape
    split_at = [int(v) for v in np.cumsum(IN_SPLITS)[:-1]]
    for l in range(DEPTH):
        h = rmsnorm(x, norm_mix_w[l])
        proj = h @ w_in[l]
        lru_x, lru_gate, ssd_z, ssd_xbc, ssd_dt = jnp.split(proj, split_at, axis=-1)
        xl = causal_depthwise_conv(lru_x, lru_conv_w[l], lru_conv_b[l])
        y_lru = rglru(xl, lru_wa[l], lru_ba[l], lru_wx[l], lru_bx[l], lru_lambda[l]) * jax.nn.gelu(lru_gate)
        xbc = jax.nn.silu(causal_depthwise_conv(ssd_xbc, ssd_conv_w[l], ssd_conv_b[l]))
        xs, bm, cm = jnp.split(xbc, [SSD_WIDTH, SSD_WIDTH + SSD_GROUPS * SSD_STATE], axis=-1)
        xs_h = xs.astype(jnp.float32).reshape(bsz, s, SSD_HEADS, SSD_HEAD_DIM)
        dt = jax.nn.softplus(ssd_dt.astype(jnp.float32) + ssd_dt_bias[l].astype(jnp.float32))
        a = -jnp.exp(ssd_a_log[l].astype(jnp.float32))
        y = ssd_chunked(xs_h, dt, a,
                        bm.astype(jnp.float32).reshape(bsz, s, SSD_GROUPS, SSD_STATE),
                        cm.astype(jnp.float32).reshape(bsz, s, SSD_GROUPS, SSD_STATE))
        y = y + ssd_d[l].astype(jnp.float32)[:, None] * xs_h
        y = y.reshape(bsz, s, SSD_WIDTH).astype(x.dtype)
        y_ssd = rmsnorm(y * jax.nn.silu(ssd_z), ssd_norm_w[l])
        x = x + jnp.concatenate([y_lru, y_ssd], axis=-1) @ w_out[l]
        h = rmsnorm(x, norm_ffn_w[l])
        x = x + peer(h, peer_wq[l], peer_sub_keys[l], peer_u[l], peer_v[l])
    return rmsnorm(x, norm_final_w)
```

```python
import math
import numpy as np
from contextlib import ExitStack
import concourse.bass as bass
import concourse.mybir as mybir
from concourse.bass_utils import run_bass_kernel_spmd

F32 = mybir.dt.float32
BF16 = mybir.dt.bfloat16
U32 = mybir.dt.uint32
AF = mybir.ActivationFunctionType
ALU = mybir.AluOpType
AX = mybir.AxisListType

ENGS = ("tensor", "vector", "scalar", "gpsimd", "sync")
PE, V, S, G, SP = "tensor", "vector", "scalar", "gpsimd", "sync"


class T:
    __slots__ = ("name", "t", "w", "r")

    def __init__(self, name, t):
        self.name = name
        self.t = t
        self.w = None
        self.r = []

    def __getitem__(self, idx):
        return self.t[idx]


class FW:
    def __init__(self, nc, es):
        self.nc = nc
        self.es = es
        self.sems = {}
        self.cnt = {}
        self.waited = {e: {} for e in ENGS}
        self.eng = {PE: nc.tensor, V: nc.vector, S: nc.scalar, G: nc.gpsimd, SP: nc.sync}
        for e in ENGS:
            self._newsem("E_" + e)
        self.dsems = []

    def _newsem(self, key):
        s = self.es.enter_context(self.nc.semaphore(key))
        self.sems[key] = s
        self.cnt[key] = 0
        return key

    def sb(self, name, shape, dtype=F32, es=None):
        self.uid = getattr(self, "uid", 0) + 1
        nm = "s%d_%s" % (self.uid, name)
        t = (es or self.es).enter_context(self.nc.sbuf_tensor(nm, list(shape), dtype))
        return T(nm, t)

    def ps(self, name, shape, dtype=F32, es=None):
        self.uid = getattr(self, "uid", 0) + 1
        t = (es or self.es).enter_context(self.nc.psum_tensor("p%d_%s" % (self.uid, name), list(shape), dtype))
        return T(name, t)

    def dmasem(self, name):
        k = self._newsem("D_" + name)
        self.dsems.append(k)
        return k

    def _need(self, eng, ev, waits):
        if ev is None:
            return
        k, v = ev
        if k == "E_" + eng and v > self.cnt[k]:
            return
        if self.waited[eng].get(k, 0) >= v:
            return
        if waits.get(k, 0) < v:
            waits[k] = v

    def _deps(self, eng, reads, writes):
        waits = {}
        for t in reads:
            self._need(eng, t.w, waits)
        for t in writes:
            self._need(eng, t.w, waits)
            for ev in t.r:
                self._need(eng, ev, waits)
        e = self.eng[eng]
        for k, v in waits.items():
            self.waited[eng][k] = v
            e.wait_ge(self.sems[k], v)

    def _mark(self, ev, reads, writes):
        for t in writes:
            t.w = ev
            t.r = []
        for t in reads:
            if t not in writes:
                t.r = [x for x in t.r if x[0] != ev[0]] + [ev]

    def op(self, eng, fn, reads=(), writes=(), inc=True):
        key = "E_" + eng
        self._deps(eng, reads, writes)
        ins = fn(self.eng[eng])
        if inc:
            self.cnt[key] += 1
            ev = (key, self.cnt[key])
            ins.then_inc(self.sems[key], 1)
        else:
            ev = (key, self.cnt[key] + 1)
        self._mark(ev, reads, writes)

    def dma(self, eng, dsem, out_ap, in_ap, reads=(), writes=(), **kw):
        self._deps(eng, reads, writes)
        self.cnt[dsem] += 16
        ev = (dsem, self.cnt[dsem])
        self.eng[eng].dma_start(out=out_ap, in_=in_ap, **kw).then_inc(self.sems[dsem], 16)
        self._mark(ev, reads, writes)

    def barrier(self):
        for eng in ENGS:
            e = self.eng[eng]
            for k, v in self.cnt.items():
                if v > 0 and self.waited[eng].get(k, 0) < v:
                    if k == "E_" + eng:
                        continue
                    self.waited[eng][k] = v
                    e.wait_ge(self.sems[k], v)


D = 2048
NTOK = 2048
SBK = 512
NSB = NTOK // SBK
NTT = SBK // 128
PBK = 256
EPS = 1e-6
NEG = -1.0e30
OHS = 12.0
OH_HOT = 1.125

O_NMW, O_NFW, O_LCW, O_LCB, O_LBA, O_LBX, O_LAM = 0, 16, 32, 64, 72, 80, 88
O_SCW, O_SCB, O_SNW, O_DTB, O_ALOG, O_SD = 96, 144, 156, 164, 180, 196
NSP = 212


def bc(ap, axis, shape):
    return ap.unsqueeze(axis).broadcast_to(list(shape))


def build_nc(nsb=NSB, do_peer=True, dbg=False):
    nc = bass.Bass("TRN2", target_bir_lowering=False)
    dt_ = nc.dram_tensor
    x_d = dt_("x", [NTOK, D], F32, kind="ExternalInput")
    sp_d = dt_("sp", [128, NSP], F32, kind="ExternalInput")
    nfin_d = dt_("nfin", [128, D], F32, kind="ExternalInput")
    win_d = dt_("win", [28, 128, 16 * 128], F32, kind="ExternalInput")
    wz_d = dt_("wz", [2, 128, 16 * 512], F32, kind="ExternalInput")
    wdt_d = dt_("wdt", [128, 16 * 16], F32, kind="ExternalInput")
    wabd_d = dt_("wabd", [128, 8 * 128], F32, kind="ExternalInput")
    wxbd_d = dt_("wxbd", [128, 8 * 128], F32, kind="ExternalInput")
    wout_d = dt_("wout", [4, 128, 16 * 512], F32, kind="ExternalInput")
    wq_d = dt_("wq", [16, 128, 16 * 128], F32, kind="ExternalInput")
    skt_d = dt_("skt", [128, 16 * 128], F32, kind="ExternalInput")
    ut_d = dt_("ut", [128, 128, 16 * 128], F32, kind="ExternalInput")
    v_d = dt_("v", [128, 128, D], F32, kind="ExternalInput")
    out_d = dt_("out", [NTOK, D], F32, kind="ExternalOutput")
    gd_d = dt_("gd", [128, 128, NTOK // 128, 128], BF16, kind="Internal")
    x2_d = dt_("x2s", [NTOK, D], F32, kind="Internal")
    h2_d = dt_("h2s", [128, 16, NTOK], BF16, kind="Internal")
    if dbg:
        dbg_d = dt_("dbg", [128, 8192], F32, kind="ExternalOutput")

    with ExitStack() as es:
        fw = FW(nc, es)
        outT = T("out", out_d)
        gdT = T("gd", gd_d)
        gdTs = [T("gd%d" % i, gd_d) for i in range(4)]
        x2Ts = [T("x2s%d" % i, x2_d) for i in range(16)]
        outTs = [T("out%d" % i, out_d) for i in range(8)]
        h2T_ = T("h2s", h2_d)
        dbg_stage = [fw.sb("dbgst%d" % i, [128, n]) for i, n in enumerate([512, 512, 512, 512, 128, 128, 128])] if dbg else []
        psum1 = ExitStack()
        p1 = ExitStack()
        _sb = fw.sb
        fw.sb = lambda name, shape, dtype=F32, es=None: _sb(name, shape, dtype, es=(es or p1))
        PF = [fw.ps("pf%d" % i, [128, 512], F32, es=psum1) for i in range(6)]
        PB = [fw.ps("pb%d" % i, [128, 1024], BF16, es=psum1) for i in range(2)]
        spk = fw.sb("spk", [128, NSP])
        identf = fw.sb("identf", [128, 128])
        identb = fw.sb("identb", [128, 128], BF16)
        triU = fw.sb("triU", [128, 128])
        triS = fw.sb("triS", [128, 128])
        ones = fw.sb("ones", [128, 128])
        iota128 = fw.sb("iota128", [128, 128])
        wabd = fw.sb("wabd", [128, 8, 128], BF16)
        wxbd = fw.sb("wxbd", [128, 8, 128], BF16)
        wdt = fw.sb("wdt", [128, 16, 16], BF16)
        skt = fw.sb("skt", [128, 16, 128], BF16)
        clam = fw.sb("clam", [128, 8])
        clam2 = fw.sb("clam2", [128, 8])
        arow = fw.sb("arow", [128, 16])
        halo = fw.sb("halo", [128, 20, 3])
        lstate = fw.sb("lstate", [128, 8])
        sstate = fw.sb("sstate", [128, 1024])
        sstate_b = fw.sb("sstate_b", [128, 1024], BF16)
        xt = [fw.sb("xt%d" % i, [128, D]) for i in range(NTT)]
        hT = fw.sb("hT", [128, 16, SBK], BF16)
        IT = fw.sb("IT", [128, SBK])
        JT = fw.sb("JT", [128, SBK])
        GT = fw.sb("GT", [128, SBK])
        NJT = fw.sb("NJT", [128, SBK])
        ssq = fw.sb("ssq", [128, 8])
        rstd = fw.sb("rstd", [128, 8])
        hb = fw.sb("hb", [128, D], BF16)

        dc_ = fw.dmasem("const")
        dcg = fw.dmasem("constg")
        dx = [fw.dmasem("x%d" % i) for i in range(NTT)]
        dout = fw.dmasem("out")
        dwb = [fw.dmasem("wb%d" % i) for i in range(3)]
        dwz = fw.dmasem("wz")
        dwo = [fw.dmasem("wo%d" % i) for i in range(2)]
        dwq = [fw.dmasem("wq%d" % i) for i in range(2)]
        du = [fw.dmasem("u%d" % i) for i in range(2)]
        dv = [fw.dmasem("v%d" % i) for i in range(2)]
        dgo = [fw.dmasem("gout%d" % i) for i in range(4)]
        dgi = [fw.dmasem("gin%d" % i) for i in range(2)]
        dx2 = [fw.dmasem("x2o%d" % i) for i in range(NTT)]
        dh2 = fw.dmasem("h2o")
        dx2i = [fw.dmasem("x2i%d" % i) for i in range(8)]
        dh2i = fw.dmasem("h2i")
        douts = [fw.dmasem("out%d" % i) for i in range(8)]

        dcs = [fw.dmasem("cst%d" % i) for i in range(4)]
        fw.dma(SP, dc_, spk[:], sp_d.ap(), writes=[spk])
        dnf = fw.dmasem("nfin")
        fw.dma(G, dcs[0], wabd[:].rearrange("p a b -> p (a b)"), wabd_d.ap(), writes=[wabd])
        fw.dma(G, dcs[1], wxbd[:].rearrange("p a b -> p (a b)"), wxbd_d.ap(), writes=[wxbd])
        fw.dma(G, dcs[2], wdt[:].rearrange("p a b -> p (a b)"), wdt_d.ap(), writes=[wdt])
        fw.dma(G, dcs[3], skt[:].rearrange("p a b -> p (a b)"), skt_d.ap(), writes=[skt])

        fw.op(G, lambda e: e.memset(ones[:], 1.0), writes=[ones])
        fw.op(G, lambda e: e.memset(identf[:], 1.0), writes=[identf])
        fw.op(G, lambda e: e.affine_select(out=identf[:], in_=identf[:], pattern=[[1, 128]], compare_op=ALU.is_equal,
                                           fill=0.0, base=0, channel_multiplier=-1), reads=[identf], writes=[identf])
        fw.op(V, lambda e: e.tensor_copy(identb[:], identf[:]), reads=[identf], writes=[identb])
        fw.op(G, lambda e: e.affine_select(out=triU[:], in_=ones[:], pattern=[[1, 128]], compare_op=ALU.is_ge,
                                           fill=0.0, base=0, channel_multiplier=-1), reads=[ones], writes=[triU])
        fw.op(G, lambda e: e.affine_select(out=triS[:], in_=ones[:], pattern=[[-1, 128]], compare_op=ALU.is_gt,
                                           fill=0.0, base=0, channel_multiplier=1), reads=[ones], writes=[triS])
        fw.op(G, lambda e: e.iota(iota128[:], pattern=[[1, 128]], base=0, channel_multiplier=0,
                                  allow_small_or_imprecise_dtypes=True), writes=[iota128])
        fw.op(V, lambda e: e.memset(halo[:], 0.0), writes=[halo])
        fw.op(V, lambda e: e.memset(lstate[:], 0.0), writes=[lstate])
        fw.op(V, lambda e: e.memset(sstate[:], 0.0), writes=[sstate])
        fw.op(V, lambda e: e.memset(sstate_b[:], 0.0), writes=[sstate_b])

        ee = fw.sb("ee", [128, 8])
        pp = fw.sb("pp", [128, 8])
        lam = spk[:, O_LAM:O_LAM + 8]
        fw.op(S, lambda e: e.activation(out=ee[:], in_=lam, func=AF.Exp, scale=-1.0), reads=[spk], writes=[ee])
        fw.op(V, lambda e: e.tensor_scalar(out=pp[:], in0=ee[:], scalar1=-0.25, scalar2=1.0 / 3.0, op0=ALU.mult, op1=ALU.add), reads=[ee], writes=[pp])
        fw.op(V, lambda e: e.tensor_tensor(out=pp[:], in0=pp[:], in1=ee[:], op=ALU.mult), reads=[pp, ee], writes=[pp])
        fw.op(V, lambda e: e.tensor_scalar(out=pp[:], in0=pp[:], scalar1=-0.5, scalar2=None, op0=ALU.add), reads=[pp], writes=[pp])
        fw.op(V, lambda e: e.tensor_tensor(out=pp[:], in0=pp[:], in1=ee[:], op=ALU.mult), reads=[pp, ee], writes=[pp])
        fw.op(V, lambda e: e.tensor_scalar(out=pp[:], in0=pp[:], scalar1=1.0, scalar2=None, op0=ALU.add), reads=[pp], writes=[pp])
        fw.op(V, lambda e: e.tensor_tensor(out=pp[:], in0=pp[:], in1=ee[:], op=ALU.mult), reads=[pp, ee], writes=[pp])
        fw.op(V, lambda e: e.tensor_scalar(out=clam[:], in0=pp[:], scalar1=-8.0, scalar2=None, op0=ALU.mult), reads=[pp], writes=[clam])
        fw.op(V, lambda e: e.tensor_scalar(out=clam2[:], in0=pp[:], scalar1=-16.0, scalar2=None, op0=ALU.mult), reads=[pp], writes=[clam2])
        fw.op(S, lambda e: e.activation(out=arow[:], in_=spk[:, O_ALOG:O_ALOG + 16], func=AF.Exp), reads=[spk], writes=[arow])
        fw.op(V, lambda e: e.tensor_scalar(out=arow[:], in0=arow[:], scalar1=-1.0, scalar2=None, op0=ALU.mult), reads=[arow], writes=[arow])

        dbg_col = [0]

        def tap(src_t, ap, n):
            if not dbg:
                return
            c0 = dbg_col[0]
            stage = T("stg", dbg_stage.pop(0).t[:, 0:n])
            fw.op(V, lambda e: e.tensor_copy(stage.t, ap), reads=[src_t], writes=[stage])
            ds_ = fw.dmasem("dbg%d" % c0)
            dT = T("dbgd%d" % c0, dbg_d)
            fw.dma(SP, ds_, dbg_d.ap()[:, c0:c0 + n], stage.t, reads=[stage], writes=[dT])
            taps.append(dT)
            dbg_col[0] += n
        taps = []

        def rms_and_transpose(tt, wcol_off, first_load):
            x_t = xt[tt]
            fw.op(S, lambda e: e.activation(out=hb[:], in_=x_t[:], func=AF.Square, accum_out=ssq[:, tt:tt + 1]),
                  reads=[x_t], writes=[hb, ssq])
            fw.op(V, lambda e: e.tensor_scalar(out=rstd[:, tt:tt + 1], in0=ssq[:, tt:tt + 1], scalar1=1.0 / D, scalar2=EPS,
                                               op0=ALU.mult, op1=ALU.add), reads=[ssq], writes=[rstd])
            fw.op(S, lambda e: e.activation(out=rstd[:, tt:tt + 1], in_=rstd[:, tt:tt + 1], func=AF.Sqrt), reads=[rstd], writes=[rstd])
            fw.op(V, lambda e: e.reciprocal(out=rstd[:, tt:tt + 1], in_=rstd[:, tt:tt + 1]), reads=[rstd], writes=[rstd])
            fw.op(V, lambda e: e.tensor_scalar(out=hb[:], in0=x_t[:], scalar1=rstd[:, tt:tt + 1], scalar2=None, op0=ALU.mult),
                  reads=[x_t, rstd], writes=[hb])
            for half in range(2):
                pb = PB[half]
                for q in range(8):
                    dcn = half * 8 + q
                    fw.op(PE, lambda e, q=q, dcn=dcn, pb=pb: e.transpose(out=pb[:, q * 128:(q + 1) * 128], in_=hb[:, dcn * 128:(dcn + 1) * 128],
                                                                      identity=identb[:]), reads=[hb, identb], writes=[pb], inc=(q == 7))
                wv = spk[:, wcol_off + half * 8: wcol_off + half * 8 + 8]
                fw.op(V, lambda e, pb=pb, wv=wv, half=half: e.tensor_tensor(
                    out=hT[:, half * 8:(half + 1) * 8, tt * 128:(tt + 1) * 128],
                    in0=pb[:].rearrange("p (a b) -> p a b", b=128), in1=bc(wv, 2, [128, 8, 128]), op=ALU.mult),
                    reads=[pb, spk], writes=[hT])

        for sbi in range(nsb):
            t0 = sbi * SBK
            for tt in range(NTT):
                fw.dma(SP, dx[tt], xt[tt][:], x_d.ap()[t0 + tt * 128: t0 + (tt + 1) * 128, :], writes=[xt[tt]])
            for tt in range(NTT):
                rms_and_transpose(tt, O_NMW, True)
            if sbi == 0:
                tap(hT, hT[:, 0, :], 512)

            bes = ExitStack()
            mixT = fw.sb("mixT", [128, 16, SBK], BF16, es=bes)
            with ExitStack() as mes:
                xsT = fw.sb("xsT", [128, 8, SBK], es=mes)
                BT = fw.sb("BT", [128, 2, SBK], BF16, es=mes)
                CT = fw.sb("CT", [128, 2, SBK], BF16, es=mes)
                zs = [fw.sb("zs%d" % i, [128, 1024], es=mes) for i in range(NTT)]
                dtk = fw.sb("dtk", [128, NTT, 16], es=mes)
                adt = fw.sb("adt", [128, NTT, 16], es=mes)
                mes1 = ExitStack()
                wbuf = [fw.sb("wbuf%d" % i, [128, 16, 128], BF16, es=mes1) for i in range(3)]
                xraw = [fw.sb("xraw%d" % i, [128, SBK + 3], es=mes1) for i in range(2)]
                xl = fw.sb("xl", [128, SBK], es=mes1)
                xlb = fw.sb("xlb", [128, SBK], BF16, es=mes1)
                rg = fw.sb("rg", [128, SBK], es=mes1)
                ig = fw.sb("ig", [128, SBK], es=mes1)
                av = fw.sb("av", [128, SBK], es=mes1)
                bv = fw.sb("bv", [128, SBK], es=mes1)
                hs = fw.sb("hs", [128, SBK], es=mes1)
                gg = fw.sb("gg", [128, SBK], es=mes1)
                wzb = fw.sb("wzb", [128, 16, 512], BF16, es=mes1)

                wq_i = [0]

                def load_w(blk):
                    i = wq_i[0] % 3
                    wq_i[0] += 1
                    wb = wbuf[i]
                    fw.dma(G, dwb[i], wb[:].rearrange("p a b -> p (a b)"), win_d.ap()[blk], writes=[wb])
                    return wb

                def proj_block(wb, pf):
                    for dcn in range(16):
                        fw.op(PE, lambda e, dcn=dcn: e.matmul(pf[:], lhsT=wb[:, dcn, :], rhs=hT[:, dcn, :], start=(dcn == 0), stop=(dcn == 15)),
                              reads=[wb, hT], writes=[pf], inc=(dcn == 15))

                def conv_block(pf, hidx, wcol, bcol, xr, out_ap, out_t, func):
                    fw.op(V, lambda e: e.tensor_copy(xr[:, 0:3], halo[:, hidx, :]), reads=[halo], writes=[xr])
                    fw.op(S, lambda e: e.copy(out=xr[:, 3:SBK + 3], in_=pf[:]), reads=[pf], writes=[xr])
                    fw.op(V, lambda e: e.tensor_copy(halo[:, hidx, :], xr[:, SBK:SBK + 3]), reads=[xr], writes=[halo])
                    w = lambda k: spk[:, wcol + k: wcol + k + 1]
                    fw.op(V, lambda e: e.tensor_scalar(out=xl[:], in0=xr[:, 3:SBK + 3], scalar1=w(3), scalar2=spk[:, bcol:bcol + 1],
                                                       op0=ALU.mult, op1=ALU.add), reads=[xr, spk], writes=[xl])
                    for k in (2, 1, 0):
                        fw.op(V, lambda e, k=k: e.scalar_tensor_tensor(out=xl[:], in0=xr[:, k:SBK + k], scalar=w(k), in1=xl[:],
                                                                       op0=ALU.mult, op1=ALU.add), reads=[xr, spk, xl], writes=[xl])
                    if func is not None:
                        fw.op(S, lambda e: e.activation(out=out_ap, in_=xl[:], func=func), reads=[xl], writes=[out_t])

                pfi = [0]

                def next_pf():
                    p = PF[pfi[0] % 3]
                    pfi[0] += 1
                    return p

                for k in range(8):
                    wbx = load_w(k)
                    wbg = load_w(8 + k)
                    pfx = next_pf()
                    proj_block(wbx, pfx)
                    xr = xraw[k % 2]
                    conv_block(pfx, k, O_LCW + 4 * k, O_LCB + k, xr, None, None, None)
                    fw.op(V, lambda e: e.tensor_copy(xlb[:], xl[:]), reads=[xl], writes=[xlb])
                    fw.op(PE, lambda e: e.matmul(PF[3][:], lhsT=wabd[:, k, :], rhs=xlb[:], start=True, stop=True), reads=[wabd, xlb], writes=[PF[3]])
                    fw.op(PE, lambda e: e.matmul(PF[4][:], lhsT=wxbd[:, k, :], rhs=xlb[:], start=True, stop=True), reads=[wxbd, xlb], writes=[PF[4]])
                    fw.op(S, lambda e: e.activation(out=rg[:], in_=PF[3][:], func=AF.Sigmoid, bias=spk[:, O_LBA + k:O_LBA + k + 1]),
                          reads=[PF[3], spk], writes=[rg])
                    fw.op(S, lambda e: e.activation(out=ig[:], in_=PF[4][:], func=AF.Sigmoid, bias=spk[:, O_LBX + k:O_LBX + k + 1]),
                          reads=[PF[4], spk], writes=[ig])
                    fw.op(S, lambda e: e.activation(out=av[:], in_=rg[:], func=AF.Exp, scale=clam[:, k:k + 1]), reads=[rg, clam], writes=[av])
                    fw.op(S, lambda e: e.activation(out=bv[:], in_=rg[:], func=AF.Exp, scale=clam2[:, k:k + 1]), reads=[rg, clam2], writes=[bv])
                    fw.op(V, lambda e: e.tensor_scalar(out=bv[:], in0=bv[:], scalar1=-1.0, scalar2=1.0, op0=ALU.mult, op1=ALU.add), reads=[bv], writes=[bv])
                    fw.op(S, lambda e: e.activation(out=bv[:], in_=bv[:], func=AF.Sqrt), reads=[bv], writes=[bv])
                    fw.op(V, lambda e: e.tensor_tensor(out=ig[:], in0=ig[:], in1=xl[:], op=ALU.mult), reads=[ig, xl], writes=[ig])
                    fw.op(V, lambda e: e.tensor_tensor(out=bv[:], in0=bv[:], in1=ig[:], op=ALU.mult), reads=[bv, ig], writes=[bv])
                    fw.op(V, lambda e: e.tensor_tensor_scan(out=hs[:], data0=av[:], data1=bv[:], initial=lstate[:, k:k + 1],
                                                            op0=ALU.mult, op1=ALU.add), reads=[av, bv, lstate], writes=[hs])
                    fw.op(V, lambda e: e.tensor_copy(lstate[:, k:k + 1], hs[:, SBK - 1:SBK]), reads=[hs], writes=[lstate])
                    pfg = next_pf()
                    proj_block(wbg, pfg)
                    fw.op(S, lambda e: e.activation(out=gg[:], in_=pfg[:], func=AF.Gelu_apprx_tanh), reads=[pfg], writes=[gg])
                    fw.op(V, lambda e: e.tensor_tensor(out=mixT[:, k, :], in0=hs[:], in1=gg[:], op=ALU.mult), reads=[hs, gg], writes=[mixT])
                if sbi == 0:
                    tap(mixT, mixT[:, 0, :], 512)

                for k in range(12):
                    wb = load_w(16 + k)
                    pf = next_pf()
                    proj_block(wb, pf)
                    xr = xraw[k % 2]
                    if k < 8:
                        conv_block(pf, 8 + k, O_SCW + 4 * k, O_SCB + k, xr, xsT[:, k, :], xsT, AF.Silu)
                    elif k < 10:
                        conv_block(pf, 8 + k, O_SCW + 4 * k, O_SCB + k, xr, BT[:, k - 8, :], BT, AF.Silu)
                    else:
                        conv_block(pf, 8 + k, O_SCW + 4 * k, O_SCB + k, xr, CT[:, k - 10, :], CT, AF.Silu)
                for half in range(2):
                    for a in range(4):
                        fw.dma(G, dwz, wzb[:, a * 4:(a + 1) * 4, :].rearrange("p a b -> p (a b)"),
                               wz_d.ap()[half][:, a * 2048:(a + 1) * 2048], writes=[wzb])
                    for tt in range(NTT):
                        pf = next_pf()
                        for dcn in range(16):
                            fw.op(PE, lambda e, dcn=dcn: e.matmul(pf[:], lhsT=hT[:, dcn, tt * 128:(tt + 1) * 128], rhs=wzb[:, dcn, :],
                                                                  start=(dcn == 0), stop=(dcn == 15)), reads=[hT, wzb], writes=[pf], inc=(dcn == 15))
                        fw.op(S, lambda e: e.activation(out=zs[tt][:, half * 512:(half + 1) * 512], in_=pf[:], func=AF.Silu), reads=[pf], writes=[zs[tt]])
                for tt in range(NTT):
                    for dcn in range(16):
                        fw.op(PE, lambda e, dcn=dcn: e.matmul(PF[5][:, 0:16], lhsT=hT[:, dcn, tt * 128:(tt + 1) * 128], rhs=wdt[:, dcn, :],
                                                              start=(dcn == 0), stop=(dcn == 15)), reads=[hT, wdt], writes=[PF[5]], inc=(dcn == 15))
                    fw.op(V, lambda e: e.tensor_tensor(out=dtk[:, tt, :], in0=PF[5][:, 0:16], in1=spk[:, O_DTB:O_DTB + 16], op=ALU.add),
                          reads=[PF[5], spk], writes=[dtk])
                fw.op(S, lambda e: e.activation(out=dtk[:], in_=dtk[:], func=AF.Exp), reads=[dtk], writes=[dtk])
                fw.op(S, lambda e: e.activation(out=dtk[:], in_=dtk[:], func=AF.Ln, bias=1.0), reads=[dtk], writes=[dtk])
                fw.op(V, lambda e: e.tensor_tensor(out=adt[:], in0=dtk[:], in1=bc(arow[:], 1, [128, NTT, 16]), op=ALU.mult),
                      reads=[dtk, arow], writes=[adt])

                fw.barrier()
                mes1.close()
                mes = ExitStack()
                csx = fw.sb("csx", [128, 16], es=mes)
                dcy = fw.sb("dcy", [128, 16], es=mes)
                dtd = fw.sb("dtd", [128, 16], es=mes)
                etot = fw.sb("etot", [128, 16], es=mes)
                Rm = fw.sb("Rm", [128, 16, 128], es=mes)
                Eb = [fw.sb("Eb%d" % i, [128, 4, 128], es=mes) for i in range(2)]
                cbm = fw.sb("cbm", [128, 2, 128], es=mes)
                scT = fw.sb("scT", [128, 16, 128], BF16, es=mes)
                xtok = fw.sb("xtok", [128, 16, 64], es=mes)
                xc = fw.sb("xc", [128, 16, 64], BF16, es=mes)
                xcd = fw.sb("xcd", [128, 16, 64], BF16, es=mes)
                btok = fw.sb("btok", [128, 2, 128], BF16, es=mes)
                yv = fw.sb("yv", [128, 16, 64], es=mes)
                y2 = fw.sb("y2", [128, 16, 64], es=mes)
                ynb = fw.sb("ynb", [128, 1024], BF16, es=mes)
                ss2 = fw.sb("ss2", [128, 1], es=mes)
                rs2 = fw.sb("rs2", [128, 1], es=mes)
                for c in range(NTT):
                    cs_ = slice(c * 128, (c + 1) * 128)
                    fw.op(PE, lambda e: e.matmul(PF[5][:, 0:16], lhsT=triU[:], rhs=adt[:, c, :], start=True, stop=True), reads=[triU, adt], writes=[PF[5]], inc=False)
                    fw.op(PE, lambda e: e.matmul(PF[5][:, 16:32], lhsT=ones[:], rhs=adt[:, c, :], start=True, stop=True), reads=[ones, adt], writes=[PF[5]])
                    fw.op(S, lambda e: e.activation(out=csx[:], in_=PF[5][:, 0:16], func=AF.Exp), reads=[PF[5]], writes=[csx])
                    fw.op(V, lambda e: e.tensor_tensor(out=dcy[:], in0=PF[5][:, 16:32], in1=csx[:], op=ALU.subtract), reads=[PF[5], csx], writes=[dcy]) if False else None
                    fw.op(V, lambda e: e.tensor_copy(dtd[:], PF[5][:, 0:16]), reads=[PF[5]], writes=[dtd])
                    fw.op(V, lambda e: e.tensor_tensor(out=dcy[:], in0=PF[5][:, 16:32], in1=dtd[:], op=ALU.subtract), reads=[PF[5], dtd], writes=[dcy])
                    fw.op(S, lambda e: e.activation(out=dcy[:], in_=dcy[:], func=AF.Exp), reads=[dcy], writes=[dcy])
                    fw.op(S, lambda e: e.activation(out=etot[:], in_=PF[5][:, 16:32], func=AF.Exp), reads=[PF[5]], writes=[etot])
                    fw.op(V, lambda e: e.tensor_tensor(out=dtd[:], in0=dtk[:, c, :], in1=dcy[:], op=ALU.mult), reads=[dtk, dcy], writes=[dtd])
                    for g in range(2):
                        fw.op(PE, lambda e, g=g: e.matmul(PF[2][:, g * 128:(g + 1) * 128], lhsT=BT[:, g, cs_], rhs=CT[:, g, cs_], start=True, stop=True),
                              reads=[BT, CT], writes=[PF[2]], inc=(g == 1))
                    fw.op(V, lambda e: e.tensor_tensor(out=cbm[:], in0=PF[2][:, 0:256].rearrange("p (a b) -> p a b", b=128),
                                                       in1=bc(triU[:], 1, [128, 2, 128]), op=ALU.mult), reads=[PF[2], triU], writes=[cbm])
                    for k in range(8):
                        pf = PF[3 + k // 4]
                        fw.op(PE, lambda e, k=k, pf=pf: e.transpose(out=pf[:, (k % 4) * 128:(k % 4 + 1) * 128], in_=xsT[:, k, cs_], identity=identf[:]),
                              reads=[xsT, identf], writes=[pf], inc=(k % 4 == 3))
                    for hh in range(2):
                        fw.op(S, lambda e, hh=hh: e.copy(out=xtok[:, hh * 8:(hh + 1) * 8, :].rearrange("p a b -> p (a b)"), in_=PF[3 + hh][:]),
                              reads=[PF[3 + hh]], writes=[xtok])
                    fw.op(V, lambda e: e.tensor_tensor(out=xc[:], in0=xtok[:], in1=bc(dtk[:, c, :], 2, [128, 16, 64]), op=ALU.mult), reads=[xtok, dtk], writes=[xc])
                    fw.op(V, lambda e: e.tensor_tensor(out=xcd[:], in0=xtok[:], in1=bc(dtd[:], 2, [128, 16, 64]), op=ALU.mult), reads=[xtok, dtd], writes=[xcd])
                    fw.op(V, lambda e: e.tensor_tensor(out=Rm[:], in0=bc(triU[:], 1, [128, 16, 128]), in1=bc(adt[:, c, :], 2, [128, 16, 128]), op=ALU.mult),
                          reads=[triU, adt], writes=[Rm])
                    for q in range(4):
                        pf = PF[q % 2]
                        fw.op(PE, lambda e, q=q, pf=pf: e.matmul(pf[:], lhsT=triS[:], rhs=Rm[:, q * 4:(q + 1) * 4, :].rearrange("p a b -> p (a b)"), start=True, stop=True),
                              reads=[triS, Rm], writes=[pf])
                        eb = Eb[q % 2]
                        fw.op(S, lambda e, pf=pf, eb=eb: e.activation(out=eb[:].rearrange("p a b -> p (a b)"), in_=pf[:], func=AF.Exp), reads=[pf], writes=[eb])
                        fw.op(V, lambda e, q=q, eb=eb: e.tensor_tensor(out=scT[:, q * 4:(q + 1) * 4, :], in0=eb[:],
                                                                    in1=bc(cbm[:, q // 2, :], 1, [128, 4, 128]), op=ALU.mult), reads=[eb, cbm], writes=[scT])
                    for h in range(16):
                        pf = PF[3 + h // 8]
                        fw.op(PE, lambda e, h=h, pf=pf: e.matmul(pf[:, (h % 8) * 64:(h % 8 + 1) * 64], lhsT=scT[:, h, :], rhs=xc[:, h, :], start=True, stop=True),
                              reads=[scT, xc], writes=[pf], inc=(h % 8 == 7))
                    for g in range(2):
                        fw.op(PE, lambda e, g=g: e.matmul(PF[g][:], lhsT=CT[:, g, cs_], rhs=sstate_b[:, g * 512:(g + 1) * 512], start=True, stop=True),
                              reads=[CT, sstate_b], writes=[PF[g]])
                    for g in range(2):
                        hsl = slice(g * 8, (g + 1) * 8)
                        fw.op(V, lambda e, g=g, hsl=hsl: e.tensor_tensor(out=yv[:, hsl, :], in0=PF[g][:].rearrange("p (a b) -> p a b", b=64),
                                                                      in1=bc(csx[:, hsl], 2, [128, 8, 64]), op=ALU.mult), reads=[PF[g], csx], writes=[yv])
                        fw.op(V, lambda e, g=g, hsl=hsl: e.tensor_tensor(out=yv[:, hsl, :], in0=PF[3 + g][:].rearrange("p (a b) -> p a b", b=64),
                                                                      in1=yv[:, hsl, :], op=ALU.add), reads=[PF[3 + g], yv], writes=[yv])
                    fw.op(V, lambda e: e.tensor_tensor(out=y2[:], in0=xtok[:], in1=bc(spk[:, O_SD:O_SD + 16], 2, [128, 16, 64]), op=ALU.mult),
                          reads=[xtok, spk], writes=[y2])
                    fw.op(V, lambda e: e.tensor_tensor(out=yv[:], in0=yv[:], in1=y2[:], op=ALU.add), reads=[yv, y2], writes=[yv])
                    for g in range(2):
                        fw.op(PE, lambda e, g=g: e.transpose(out=PB[0][:, g * 128:(g + 1) * 128], in_=BT[:, g, cs_], identity=identb[:]),
                              reads=[BT, identb], writes=[PB[0]], inc=(g == 1))
                    fw.op(S, lambda e: e.copy(out=btok[:].rearrange("p a b -> p (a b)"), in_=PB[0][:, 0:256]), reads=[PB[0]], writes=[btok])
                    for g in range(2):
                        fw.op(PE, lambda e, g=g: e.matmul(PF[g][:], lhsT=btok[:, g, :], rhs=xcd[:, g * 8:(g + 1) * 8, :].rearrange("p a b -> p (a b)"), start=True, stop=True),
                              reads=[btok, xcd], writes=[PF[g]])
                    fw.op(V, lambda e: e.tensor_tensor(out=sstate[:].rearrange("p (a b) -> p a b", b=64), in0=sstate[:].rearrange("p (a b) -> p a b", b=64),
                                                       in1=bc(etot[:], 2, [128, 16, 64]), op=ALU.mult), reads=[sstate, etot], writes=[sstate])
                    for g in range(2):
                        fw.op(V, lambda e, g=g: e.tensor_tensor(out=sstate[:, g * 512:(g + 1) * 512], in0=PF[g][:], in1=sstate[:, g * 512:(g + 1) * 512], op=ALU.add),
                              reads=[PF[g], sstate], writes=[sstate])
                    fw.op(S, lambda e: e.copy(out=sstate_b[:], in_=sstate[:]), reads=[sstate], writes=[sstate_b])
                    yflat = yv[:].rearrange("p a b -> p (a b)")
                    fw.op(V, lambda e: e.tensor_tensor(out=yflat, in0=yflat, in1=zs[c][:], op=ALU.mult), reads=[yv, zs[c]], writes=[yv])
                    fw.op(S, lambda e: e.activation(out=ynb[:], in_=yflat, func=AF.Square, accum_out=ss2[:]), reads=[yv], writes=[ynb, ss2])
                    fw.op(V, lambda e: e.tensor_scalar(out=rs2[:], in0=ss2[:], scalar1=1.0 / 1024, scalar2=EPS, op0=ALU.mult, op1=ALU.add), reads=[ss2], writes=[rs2])
                    fw.op(S, lambda e: e.activation(out=rs2[:], in_=rs2[:], func=AF.Sqrt), reads=[rs2], writes=[rs2])
                    fw.op(V, lambda e: e.reciprocal(out=rs2[:], in_=rs2[:]), reads=[rs2], writes=[rs2])
                    fw.op(V, lambda e: e.tensor_scalar(out=ynb[:], in0=yflat, scalar1=rs2[:], scalar2=None, op0=ALU.mult), reads=[yv, rs2], writes=[ynb])
                    for k in range(8):
                        fw.op(PE, lambda e, k=k: e.transpose(out=PB[1][:, k * 128:(k + 1) * 128], in_=ynb[:, k * 128:(k + 1) * 128], identity=identb[:]),
                              reads=[ynb, identb], writes=[PB[1]], inc=(k == 7))
                    fw.op(V, lambda e: e.tensor_tensor(out=mixT[:, 8:16, cs_], in0=PB[1][:].rearrange("p (a b) -> p a b", b=128),
                                                       in1=bc(spk[:, O_SNW:O_SNW + 8], 2, [128, 8, 128]), op=ALU.mult), reads=[PB[1], spk], writes=[mixT])
                if sbi == 0:
                    tap(mixT, mixT[:, 8, :], 512)

                fw.barrier()
                mes.close()
                mes = ExitStack()
                wob = [fw.sb("wob%d" % i, [128, 16, 512], BF16, es=mes) for i in range(2)]
                for nb in range(4):
                    wo = wob[nb % 2]
                    for a in range(4):
                        fw.dma(G, dwo[nb % 2], wo[:, a * 4:(a + 1) * 4, :].rearrange("p a b -> p (a b)"),
                               wout_d.ap()[nb][:, a * 2048:(a + 1) * 2048], writes=[wo])
                    for tt in range(NTT):
                        pf = next_pf()
                        for cc in range(16):
                            fw.op(PE, lambda e, cc=cc: e.matmul(pf[:], lhsT=mixT[:, cc, tt * 128:(tt + 1) * 128], rhs=wo[:, cc, :],
                                                                start=(cc == 0), stop=(cc == 15)), reads=[mixT, wo], writes=[pf], inc=(cc == 15))
                        fw.op(V, lambda e: e.tensor_tensor(out=xt[tt][:, nb * 512:(nb + 1) * 512], in0=pf[:], in1=xt[tt][:, nb * 512:(nb + 1) * 512], op=ALU.add),
                              reads=[pf, xt[tt]], writes=[xt[tt]])
                if sbi == 0:
                    tap(xt[0], xt[0][:, 0:512], 512)
                fw.barrier()
                mes.close()
            for tt in range(NTT):
                fw.dma(SP, dx2[tt], x2_d.ap()[t0 + tt * 128: t0 + (tt + 1) * 128, :], xt[tt][:], reads=[xt[tt]], writes=[x2Ts[sbi * NTT + tt]])
            if do_peer:
                for tt in range(NTT):
                    rms_and_transpose(tt, O_NFW, False)
                fw.dma(SP, dh2, h2_d.ap()[:, :, t0:t0 + SBK], hT[:], reads=[hT], writes=[h2T_])
                with ExitStack() as pes:
                    peer_route_gbuild(fw, pes, sbi, locals())
                    fw.barrier()
                bes.close()
            else:
                bes.close()
        fw.barrier()
        p1.close()
        psum1.close()
        fw.sb = _sb
        if True:
            TB2 = 1024
            NT2 = TB2 // 128
            JG = 4
            PA = [fw.ps("pa%d" % i, [128, 512], F32) for i in range(4)]
            PO = [fw.ps("po%d" % i, [128, 512], F32) for i in range(3)]
            PG = fw.ps("pg", [128, 1024], BF16)
            identb2 = fw.sb("identb2", [128, 128], BF16)
            identf2 = fw.sb("identf2", [128, 128])
            fw.op(G, lambda e: e.memset(identf2[:], 1.0), writes=[identf2])
            fw.op(G, lambda e: e.affine_select(out=identf2[:], in_=identf2[:], pattern=[[1, 128]], compare_op=ALU.is_equal,
                                               fill=0.0, base=0, channel_multiplier=-1), reads=[identf2], writes=[identf2])
            fw.op(V, lambda e: e.tensor_copy(identb2[:], identf2[:]), reads=[identf2], writes=[identb2])
            acc = [fw.sb("acc%d" % i, [128, D]) for i in range(NT2)]
            h2 = fw.sb("h2", [128, 16, TB2], BF16)
            vb = [fw.sb("vb%d" % i, [128, JG, D], BF16) for i in range(2)]
            ub = [fw.sb("ub%d" % i, [128, 16 * 128], BF16) for i in range(2)]
            gb = [fw.sb("gb%d" % i, [128, NT2, 128], BF16) for i in range(2)]
            actb = [fw.sb("actb%d" % i, [128, 512]) for i in range(2)]
            wgs = [fw.sb("wg%d" % i, [128, JG, TB2], BF16) for i in range(2)]
            nfin = fw.sb("nfin", [128, D])
            jk = fw.sb("jk", [128, D], BF16)
            ssq2 = fw.sb("ssq2", [128, NT2])
            rstd2 = fw.sb("rstd2", [128, NT2])
            fw.dma(SP, dnf, nfin[:], nfin_d.ap(), writes=[nfin])
            for sb2 in range((nsb * SBK) // TB2 if nsb * SBK >= TB2 else 1):
                tb0 = sb2 * TB2
                ntl = min(NT2, (nsb * SBK - tb0) // 128)
                ncol = ntl * 128
                for tl in range(ntl):
                    fw.dma(SP, dx2i[tl], acc[tl][:], x2_d.ap()[tb0 + tl * 128: tb0 + (tl + 1) * 128, :], reads=[x2Ts[tb0 // 128 + tl]], writes=[acc[tl]])
                if do_peer:
                    fw.dma(SP, dh2i, h2[:, :, 0:ncol], h2_d.ap()[:, :, tb0:tb0 + ncol], reads=[h2T_], writes=[h2])
                    nhalf = (ncol + 511) // 512
                    for jg in range(128 // JG):
                        v_ = vb[jg % 2]
                        fw.dma(G, dv[jg % 2], v_[:], v_d.ap()[jg * JG:(jg + 1) * JG].rearrange("j p f -> p j f"), writes=[v_])
                        wg = wgs[jg % 2]
                        for jj in range(JG):
                            j = jg * JG + jj
                            u_ = ub[j % 2]
                            fw.dma(G, du[j % 2], u_[:], ut_d.ap()[j], writes=[u_])
                            g_ = gb[j % 2]
                            fw.dma(SP, dgi[j % 2], g_[:, 0:ntl, :], gd_d.ap()[j][:, tb0 // 128: tb0 // 128 + ntl, :], reads=[gdT], writes=[g_])
                            for tl in range(ntl):
                                fw.op(PE, lambda e, tl=tl: e.transpose(out=PG[:, tl * 128:(tl + 1) * 128], in_=g_[:, tl, :], identity=identb2[:]),
                                      reads=[g_, identb2], writes=[PG], inc=(tl == ntl - 1))
                            for hf in range(nhalf):
                                pa = PA[(j % 2) * 2 + hf]
                                for dcn in range(16):
                                    fw.op(PE, lambda e, dcn=dcn: e.matmul(pa[:], lhsT=u_[:, dcn * 128:(dcn + 1) * 128], rhs=h2[:, dcn, hf * 512:(hf + 1) * 512],
                                                                          start=(dcn == 0), stop=(dcn == 15)), reads=[u_, h2], writes=[pa], inc=(dcn == 15))
                                ab = actb[hf]
                                fw.op(S, lambda e: e.activation(out=ab[:], in_=pa[:], func=AF.Gelu_apprx_tanh), reads=[pa], writes=[ab])
                                fw.op(V, lambda e: e.tensor_tensor(out=wg[:, jj, hf * 512:(hf + 1) * 512], in0=ab[:], in1=PG[:, hf * 512:(hf + 1) * 512], op=ALU.mult),
                                      reads=[ab, PG], writes=[wg])
                        k = 0
                        for tl in range(ntl):
                            for db in range(4):
                                po = PO[k % 3]
                                k += 1
                                for jj in range(JG):
                                    fw.op(PE, lambda e, jj=jj: e.matmul(po[:], lhsT=wg[:, jj, tl * 128:(tl + 1) * 128], rhs=v_[:, jj, db * 512:(db + 1) * 512],
                                                                        start=(jj == 0), stop=(jj == JG - 1)), reads=[wg, v_], writes=[po], inc=(jj == JG - 1))
                                fw.op(V, lambda e: e.tensor_tensor(out=acc[tl][:, db * 512:(db + 1) * 512], in0=po[:], in1=acc[tl][:, db * 512:(db + 1) * 512], op=ALU.add),
                                      reads=[po, acc[tl]], writes=[acc[tl]])
                for tl in range(ntl):
                    x_t = acc[tl]
                    fw.op(S, lambda e: e.activation(out=jk[:], in_=x_t[:], func=AF.Square, accum_out=ssq2[:, tl:tl + 1]), reads=[x_t], writes=[jk, ssq2])
                    fw.op(V, lambda e: e.tensor_scalar(out=rstd2[:, tl:tl + 1], in0=ssq2[:, tl:tl + 1], scalar1=1.0 / D, scalar2=EPS, op0=ALU.mult, op1=ALU.add),
                          reads=[ssq2], writes=[rstd2])
                    fw.op(S, lambda e: e.activation(out=rstd2[:, tl:tl + 1], in_=rstd2[:, tl:tl + 1], func=AF.Sqrt), reads=[rstd2], writes=[rstd2])
                    fw.op(V, lambda e: e.reciprocal(out=rstd2[:, tl:tl + 1], in_=rstd2[:, tl:tl + 1]), reads=[rstd2], writes=[rstd2])
                    fw.op(V, lambda e: e.scalar_tensor_tensor(out=x_t[:], in0=x_t[:], scalar=rstd2[:, tl:tl + 1], in1=nfin[:], op0=ALU.mult, op1=ALU.mult),
                          reads=[x_t, rstd2, nfin], writes=[x_t])
                    fw.dma(SP, douts[tl], out_d.ap()[tb0 + tl * 128: tb0 + (tl + 1) * 128, :], x_t[:], reads=[x_t], writes=[outTs[tl]])
        waits = {}
        for t in outTs + taps:
            fw._need(SP, t.w, waits)
        for k, v in waits.items():
            nc.sync.wait_ge(fw.sems[k], v)
    return nc


def split_t(t, n):
    return [T("%s_%d" % (t.name, i), t.t) for i in range(n)]


def peer_route_gbuild(fw, pes, sbi, L):
    PF, PB, hT, mixT = L["PF"], L["PB"], L["hT"], L["mixT"]
    skt, identf, iota128 = L["skt"], L["identf"], L["iota128"]
    wq_d, gd_d, gdTs, dgo = L["wq_d"], L["gd_d"], L["gdTs"], L["dgo"]
    IT, JT, GT, dwq = L["IT"], L["JT"], L["GT"], L["dwq"]
    NJT = L["NJT"]
    tap = L["tap"]
    qT = mixT
    wqb = [fw.sb("wqb%d" % i, [128, 16, 128], BF16, es=pes) for i in range(2)]
    for g in range(16):
        wb = wqb[g % 2]
        fw.dma(G, dwq[g % 2], wb[:].rearrange("p a b -> p (a b)"), wq_d.ap()[g], writes=[wb])
        pf = PF[g % 2]
        for dcn in range(16):
            fw.op(PE, lambda e, dcn=dcn: e.matmul(pf[:], lhsT=wb[:, dcn, :], rhs=hT[:, dcn, :], start=(dcn == 0), stop=(dcn == 15)),
                  reads=[wb, hT], writes=[pf], inc=(dcn == 15))
        fw.op(S, lambda e: e.copy(out=qT[:, g, :], in_=pf[:]), reads=[pf], writes=[qT])
    stop_ = fw.sb("stop", [128, 16, 16], es=pes)
    itop = fw.sb("itop", [128, 16, 16], U32, es=pes)
    itopf = fw.sb("itopf", [128, 16, 16], es=pes)
    scr1 = [fw.sb("scr1_%d" % i, [128, 128], es=pes) for i in range(16)]
    scr2 = [fw.sb("scr2_%d" % i, [128, 256], es=pes) for i in range(8)]
    cand = fw.sb("cand", [128, 8, 256], es=pes)
    tv = fw.sb("tv", [128, 8, 16], es=pes)
    pos = fw.sb("pos", [128, 8, 16], U32, es=pes)
    posa = fw.sb("posa", [128, 8, 16], U32, es=pes)
    posf = fw.sb("posf", [128, 8, 16], es=pes)
    asel = fw.sb("asel", [128, 8, 16], es=pes)
    bsel = fw.sb("bsel", [128, 8, 16], es=pes)
    ohA = fw.sb("ohA", [128, 8, 16, 16], es=pes)
    ohB = fw.sb("ohB", [128, 8, 16, 16], es=pes)
    Itok = fw.sb("Itok", [128, 128], es=pes)
    Jtok = fw.sb("Jtok", [128, 128], es=pes)
    gtok = fw.sb("gtok", [128, 8, 16], es=pes)
    zsum = fw.sb("zsum", [128, 8], es=pes)
    stop_g = split_t(stop_, 16)
    itop_g = split_t(itop, 16)
    cand_h = split_t(cand, 8)
    tv_h = split_t(tv, 8)
    pos_h = split_t(pos, 8)
    PFq = [PF[g // 4] for g in range(16)]
    iota16 = iota128[:, 0:16]
    TB_ = 8
    NOB = 3
    ohI = [fw.sb("ohI%d" % i, [128, TB_, 128], BF16, es=pes) for i in range(NOB)]
    ohJ = [fw.sb("ohJ%d" % i, [128, TB_, 128], BF16, es=pes) for i in range(NOB)]
    ohIq = [split_t(o, TB_) for o in ohI]
    ohJq = [split_t(o, TB_) for o in ohJ]
    NGC = 4
    Gst = [fw.sb("Gst%d" % i, [128, 32, 128], BF16, es=pes) for i in range(NGC)]
    GPS = [(PF[5], PF[5][:]), (PB[0], PB[0][:].bitcast(F32)), (PB[1], PB[1][:].bitcast(F32))]
    cnt = {"ob": 0, "gc": 0, "gp": 0}

    def route_tile(tt):
        ts_ = slice(tt * 128, (tt + 1) * 128)
        for g in range(16):
            pq = PFq[g]
            yield fw.op(PE, lambda e, g=g: e.matmul(PF[g // 4][:, (g % 4) * 128:(g % 4 + 1) * 128], lhsT=qT[:, g, ts_], rhs=skt[:, g, :], start=True, stop=True),
                  reads=[qT, skt], writes=[pq], inc=(g % 4 == 3))
        sc = lambda g: PF[g // 4][:, (g % 4) * 128:(g % 4 + 1) * 128]
        for g in range(16):
            yield fw.op(V, lambda e, g=g: e.max(out=stop_[:, g, 0:8], in_=sc(g)), reads=[PFq[g]], writes=[stop_g[g]])
        for g in range(16):
            yield fw.op(V, lambda e, g=g: e.match_replace(out=scr1[g][:], in_to_replace=stop_[:, g, 0:8], in_values=sc(g), imm_value=NEG),
                  reads=[PFq[g], stop_g[g]], writes=[scr1[g]])
        for g in range(16):
            yield fw.op(V, lambda e, g=g: e.max_index(out=itop[:, g, 0:8], in_max=stop_[:, g, 0:8], in_values=sc(g)), reads=[PFq[g], stop_g[g]], writes=[itop_g[g]])
        for g in range(16):
            yield fw.op(V, lambda e, g=g: e.max(out=stop_[:, g, 8:16], in_=scr1[g][:]), reads=[scr1[g], stop_g[g]], writes=[stop_g[g]])
        for g in range(16):
            yield fw.op(V, lambda e, g=g: e.max_index(out=itop[:, g, 8:16], in_max=stop_[:, g, 8:16], in_values=sc(g)), reads=[PFq[g], stop_g[g], itop_g[g]], writes=[itop_g[g]])
        yield fw.op(V, lambda e: e.tensor_copy(itopf[:], itop[:]), reads=itop_g, writes=[itopf])
        for h in range(8):
            yield fw.op(V, lambda e, h=h: e.tensor_tensor(out=cand[:, h, :].rearrange("p (a b) -> p a b", b=16), in0=bc(stop_[:, 2 * h, :], 2, [128, 16, 16]),
                                                    in1=bc(stop_[:, 2 * h + 1, :], 1, [128, 16, 16]), op=ALU.add),
                  reads=[stop_g[2 * h], stop_g[2 * h + 1]], writes=[cand_h[h]])
        cd = lambda h: cand[:, h, :]
        for h in range(8):
            yield fw.op(V, lambda e, h=h: e.max(out=tv[:, h, 0:8], in_=cd(h)), reads=[cand_h[h]], writes=[tv_h[h]])
        for h in range(8):
            yield fw.op(V, lambda e, h=h: e.match_replace(out=scr2[h][:], in_to_replace=tv[:, h, 0:8], in_values=cd(h), imm_value=NEG),
                  reads=[cand_h[h], tv_h[h]], writes=[scr2[h]])
        for h in range(8):
            yield fw.op(V, lambda e, h=h: e.max_index(out=pos[:, h, 0:8], in_max=tv[:, h, 0:8], in_values=cd(h)), reads=[cand_h[h], tv_h[h]], writes=[pos_h[h]])
        for h in range(8):
            yield fw.op(V, lambda e, h=h: e.max(out=tv[:, h, 8:16], in_=scr2[h][:]), reads=[scr2[h], tv_h[h]], writes=[tv_h[h]])
        for h in range(8):
            yield fw.op(V, lambda e, h=h: e.max_index(out=pos[:, h, 8:16], in_max=tv[:, h, 8:16], in_values=cd(h)), reads=[cand_h[h], tv_h[h], pos_h[h]], writes=[pos_h[h]])
        yield fw.op(V, lambda e: e.tensor_copy(posf[:], pos[:]), reads=pos_h, writes=[posf])
        yield fw.op(V, lambda e: e.tensor_single_scalar(out=posa[:], in_=pos[:], scalar=4, op=ALU.logical_shift_right), reads=pos_h, writes=[posa])
        yield fw.op(V, lambda e: e.tensor_copy(asel[:], posa[:]), reads=[posa], writes=[asel])
        yield fw.op(V, lambda e: e.scalar_tensor_tensor(out=bsel[:], in0=asel[:], scalar=-16.0, in1=posf[:], op0=ALU.mult, op1=ALU.add),
              reads=[asel, posf], writes=[bsel])
        it4 = itopf[:].rearrange("p (h k) a -> p h k a", k=2)
        io4 = bc(bc(iota16, 1, [128, 16, 16]), 1, [128, 8, 16, 16])
        yield fw.op(V, lambda e: e.tensor_tensor(out=ohA[:], in0=bc(asel[:], 3, [128, 8, 16, 16]), in1=io4, op=ALU.is_equal), reads=[asel, iota128], writes=[ohA])
        yield fw.op(V, lambda e: e.tensor_tensor(out=ohB[:], in0=bc(bsel[:], 3, [128, 8, 16, 16]), in1=io4, op=ALU.is_equal), reads=[bsel, iota128], writes=[ohB])
        yield fw.op(V, lambda e: e.tensor_tensor(out=ohA[:], in0=ohA[:], in1=bc(it4[:, :, 0, :], 2, [128, 8, 16, 16]), op=ALU.mult), reads=[ohA, itopf], writes=[ohA])
        yield fw.op(V, lambda e: e.tensor_tensor(out=ohB[:], in0=ohB[:], in1=bc(it4[:, :, 1, :], 2, [128, 8, 16, 16]), op=ALU.mult), reads=[ohB, itopf], writes=[ohB])
        yield fw.op(V, lambda e: e.tensor_reduce(out=Itok[:].rearrange("p (h k) -> p h k", k=16), in_=ohA[:], axis=AX.X, op=ALU.add), reads=[ohA], writes=[Itok])
        yield fw.op(V, lambda e: e.tensor_reduce(out=Jtok[:].rearrange("p (h k) -> p h k", k=16), in_=ohB[:], axis=AX.X, op=ALU.add), reads=[ohB], writes=[Jtok])
        yield fw.op(V, lambda e: e.tensor_tensor(out=gtok[:], in0=tv[:], in1=bc(tv[:, :, 0], 2, [128, 8, 16]), op=ALU.subtract), reads=tv_h, writes=[gtok])
        yield fw.op(S, lambda e: e.activation(out=gtok[:], in_=gtok[:], func=AF.Exp), reads=[gtok], writes=[gtok])
        yield fw.op(V, lambda e: e.tensor_reduce(out=zsum[:], in_=gtok[:], axis=AX.X, op=ALU.add), reads=[gtok], writes=[zsum])
        yield fw.op(V, lambda e: e.reciprocal(out=zsum[:], in_=zsum[:]), reads=[zsum], writes=[zsum])
        yield fw.op(V, lambda e: e.tensor_scalar(out=zsum[:], in0=zsum[:], scalar1=1.0 / OH_HOT, scalar2=None, op0=ALU.mult), reads=[zsum], writes=[zsum])
        yield fw.op(V, lambda e: e.tensor_tensor(out=gtok[:], in0=gtok[:], in1=bc(zsum[:], 2, [128, 8, 16]), op=ALU.mult), reads=[gtok, zsum], writes=[gtok])
        for i, (src, srcap) in enumerate(((Itok, Itok[:]), (Jtok, Jtok[:]), (gtok, gtok[:].rearrange("p a b -> p (a b)")))):
            yield fw.op(PE, lambda e, i=i, srcap=srcap: e.transpose(out=PF[4][:, i * 128:(i + 1) * 128], in_=srcap, identity=identf[:]),
                  reads=[src, identf], writes=[PF[4]], inc=(i == 2))
        yield fw.op(S, lambda e: e.copy(out=IT[:, ts_], in_=PF[4][:, 0:128]), reads=[PF[4]], writes=[IT])
        yield fw.op(S, lambda e: e.copy(out=JT[:, ts_], in_=PF[4][:, 128:256]), reads=[PF[4]], writes=[JT])
        yield fw.op(S, lambda e: e.activation(out=NJT[:, ts_], in_=PF[4][:, 128:256], func=AF.Copy, scale=-OHS), reads=[PF[4]], writes=[NJT])
        yield fw.op(S, lambda e: e.copy(out=GT[:, ts_], in_=PF[4][:, 256:384]), reads=[PF[4]], writes=[GT])

    def gbuild_tile(tt, nxt=None, kstep=18):
        tile_g = sbi * NTT + tt
        for tb in range(128 // TB_):
            oi = ohI[cnt["ob"] % NOB]
            oiq = ohIq[cnt["ob"] % NOB]
            oj = ohJ[cnt["ob"] % NOB]
            ojq = ohJq[cnt["ob"] % NOB]
            cnt["ob"] += 1
            cs0 = tt * 128 + tb * TB_
            fw.op(V, lambda e: e.tensor_tensor(out=oi[:], in0=bc(iota128[:], 1, [128, TB_, 128]), in1=bc(IT[:, cs0:cs0 + TB_], 2, [128, TB_, 128]),
                                               op=ALU.is_equal), reads=[iota128, IT], writes=oiq)
            for q in range(TB_):
                fw.op(S, lambda e, q=q: e.activation(out=oj[:, q, :], in_=iota128[:], func=AF.Derivative_Erf, scale=OHS, bias=NJT[:, cs0 + q:cs0 + q + 1]),
                      reads=[iota128, NJT], writes=[ojq[q]])
            fw.op(G, lambda e: e.tensor_tensor(out=oi[:], in0=oi[:], in1=bc(GT[:, cs0:cs0 + TB_], 2, [128, TB_, 128]), op=ALU.mult),
                  reads=oiq + [GT], writes=oiq)
            if nxt is not None:
                for _ in range(kstep):
                    next(nxt, None)
            for q in range(TB_):
                t = tb * TB_ + q
                if t % 4 == 0:
                    gpT, gpap = GPS[cnt["gp"] % 3]
                    cnt["gp"] += 1
                fw.op(PE, lambda e, t=t, q=q, gpap=gpap: e.matmul(gpap[:, (t % 4) * 128:(t % 4 + 1) * 128], lhsT=oj[:, q, :], rhs=oi[:, q, :], start=True, stop=True),
                      reads=[oiq[q], ojq[q]], writes=[gpT], inc=(t % 4 == 3))
                if t % 32 == 0:
                    gci = cnt["gc"] % NGC
                    gs = Gst[gci]
                    cnt["gc"] += 1
                if t % 4 == 3:
                    tq = (t % 32) - 3
                    fw.op(S, lambda e, gpap=gpap, tq=tq, gs=gs: e.copy(out=gs[:, tq:tq + 4, :].rearrange("p t i -> p (t i)"), in_=gpap),
                          reads=[gpT], writes=[gs])
                if t % 32 == 31:
                    tl0 = t - 31
                    for jh in range(2):
                        fw.dma(SP, dgo[gci], gd_d.ap()[jh * 64:(jh + 1) * 64, tl0:tl0 + 32, tile_g, :], gs[jh * 64:(jh + 1) * 64, :, :], reads=[gs], writes=[gdTs[gci]])

    for _ in route_tile(0):
        pass
    for tt in range(NTT):
        nxt = route_tile(tt + 1) if tt + 1 < NTT else None
        gbuild_tile(tt, nxt)
        if nxt is not None:
            for _ in nxt:
                pass
    if sbi == 0:
        tap(IT, IT[:, 0:128], 128)
        tap(JT, JT[:, 0:128], 128)
        tap(GT, GT[:, 0:128], 128)


def _lay_k(w, ncols_blk):
    C = w.shape[1]
    nb = C // ncols_blk
    a = w.reshape(16, 128, nb, ncols_blk).transpose(2, 1, 0, 3)
    return np.ascontiguousarray(a).reshape(nb, 128, 16 * ncols_blk)


def prepare_inputs(inp):
    f = lambda a: np.asarray(a, dtype=np.float32)
    w_in = f(inp["w_in"])[0]
    w128 = np.concatenate([w_in[:, 0:2048], w_in[:, 3072:4608]], axis=1)
    win = _lay_k(w128, 128)
    wz = _lay_k(w_in[:, 2048:3072], 512)
    wdt = _lay_k(w_in[:, 4608:4624], 16)[0]
    wout = _lay_k(f(inp["w_out"])[0], 512)
    wq = _lay_k(f(inp["peer_wq"])[0], 128)
    sp = np.zeros((128, NSP), np.float32)
    col = lambda v, n: f(v).reshape(n, 128).T
    sp[:, O_NMW:O_NMW + 16] = col(inp["norm_mix_w"][0], 16)
    sp[:, O_NFW:O_NFW + 16] = col(inp["norm_ffn_w"][0], 16)
    lcw = f(inp["lru_conv_w"])[0]
    sp[:, O_LCW:O_LCW + 32] = lcw.reshape(4, 8, 128).transpose(2, 1, 0).reshape(128, 32)
    sp[:, O_LCB:O_LCB + 8] = col(inp["lru_conv_b"][0], 8)
    sp[:, O_LBA:O_LBA + 8] = col(inp["lru_ba"][0], 8)
    sp[:, O_LBX:O_LBX + 8] = col(inp["lru_bx"][0], 8)
    sp[:, O_LAM:O_LAM + 8] = col(inp["lru_lambda"][0], 8)
    scw = f(inp["ssd_conv_w"])[0]
    sp[:, O_SCW:O_SCW + 48] = scw.reshape(4, 12, 128).transpose(2, 1, 0).reshape(128, 48)
    sp[:, O_SCB:O_SCB + 12] = col(inp["ssd_conv_b"][0], 12)
    sp[:, O_SNW:O_SNW + 8] = col(inp["ssd_norm_w"][0], 8)
    sp[:, O_DTB:O_DTB + 16] = np.broadcast_to(f(inp["ssd_dt_bias"])[0][None, :], (128, 16))
    sp[:, O_ALOG:O_ALOG + 16] = np.broadcast_to(f(inp["ssd_a_log"])[0][None, :], (128, 16))
    sp[:, O_SD:O_SD + 16] = np.broadcast_to(f(inp["ssd_d"])[0][None, :], (128, 16))
    nfin = np.ascontiguousarray(np.broadcast_to(f(inp["norm_final_w"])[None, :], (128, D)))

    def bd(w):
        o = np.zeros((128, 8, 128), np.float32)
        for k in range(8):
            o[0:64, k, 0:64] = w[2 * k]
            o[64:128, k, 64:128] = w[2 * k + 1]
        return o.reshape(128, 8 * 128)
    wabd = bd(f(inp["lru_wa"])[0])
    wxbd = bd(f(inp["lru_wx"])[0])
    sk = f(inp["peer_sub_keys"])[0]
    skt = np.ascontiguousarray(sk.reshape(16, 128, 128).transpose(2, 0, 1)).reshape(128, 16 * 128)
    u = f(inp["peer_u"])[0]
    ut = np.ascontiguousarray(u.reshape(128, 128, 16, 128).transpose(1, 3, 2, 0)).reshape(128, 128, 16 * 128)
    vv = f(inp["peer_v"])[0]
    vl = np.ascontiguousarray(vv.reshape(128, 128, D).transpose(1, 0, 2))
    shared = dict(sp=sp, nfin=nfin, win=win, wz=wz, wdt=wdt, wabd=wabd, wxbd=wxbd, wout=wout, wq=wq, skt=skt, ut=ut, v=vl)
    return shared


_NC_CACHE = {}


def kernel(**inputs):
    x = np.asarray(inputs["x"], dtype=np.float32)
    shared = prepare_inputs(inputs)
    if "nc" not in _NC_CACHE:
        _NC_CACHE["nc"] = build_nc()
    nc = _NC_CACHE["nc"]
    in_maps = [dict(shared, x=np.ascontiguousarray(x[c])) for c in range(8)]
    res = run_bass_kernel_spmd(nc, in_maps, core_ids=list(range(8)))
    return np.stack([np.asarray(r["out"], dtype=np.float32) for r in res.results], axis=0)
```

```python
import math
import numpy as np
from contextlib import ExitStack
import concourse.bass as bass
import concourse.mybir as mybir
from concourse.bass_utils import run_bass_kernel_spmd

F32 = mybir.dt.float32
BF16 = mybir.dt.bfloat16
U32 = mybir.dt.uint32
AF = mybir.ActivationFunctionType
ALU = mybir.AluOpType
AX = mybir.AxisListType

ENGS = ("tensor", "vector", "scalar", "gpsimd", "sync")
PE, V, S, G, SP = "tensor", "vector", "scalar", "gpsimd", "sync"


class T:
    __slots__ = ("name", "t", "w", "r")

    def __init__(self, name, t):
        self.name = name
        self.t = t
        self.w = None
        self.r = []

    def __getitem__(self, idx):
        return self.t[idx]


class FW:
    def __init__(self, nc, es):
        self.nc = nc
        self.es = es
        self.sems = {}
        self.cnt = {}
        self.waited = {e: {} for e in ENGS}
        self.eng = {PE: nc.tensor, V: nc.vector, S: nc.scalar, G: nc.gpsimd, SP: nc.sync}
        for e in ENGS:
            self._newsem("E_" + e)
        self.dsems = []

    def _newsem(self, key):
        s = self.es.enter_context(self.nc.semaphore(key))
        self.sems[key] = s
        self.cnt[key] = 0
        return key

    def sb(self, name, shape, dtype=F32, es=None):
        self.uid = getattr(self, "uid", 0) + 1
        nm = "s%d_%s" % (self.uid, name)
        t = (es or self.es).enter_context(self.nc.sbuf_tensor(nm, list(shape), dtype))
        return T(nm, t)

    def ps(self, name, shape, dtype=F32, es=None):
        self.uid = getattr(self, "uid", 0) + 1
        t = (es or self.es).enter_context(self.nc.psum_tensor("p%d_%s" % (self.uid, name), list(shape), dtype))
        return T(name, t)

    def dmasem(self, name):
        k = self._newsem("D_" + name)
        self.dsems.append(k)
        return k

    def _need(self, eng, ev, waits):
        if ev is None:
            return
        k, v = ev
        if k == "E_" + eng and v > self.cnt[k]:
            return
        if self.waited[eng].get(k, 0) >= v:
            return
        if waits.get(k, 0) < v:
            waits[k] = v

    def _deps(self, eng, reads, writes):
        waits = {}
        for t in reads:
            self._need(eng, t.w, waits)
        for t in writes:
            self._need(eng, t.w, waits)
            for ev in t.r:
                self._need(eng, ev, waits)
        e = self.eng[eng]
        for k, v in waits.items():
            self.waited[eng][k] = v
            e.wait_ge(self.sems[k], v)

    def _mark(self, ev, reads, writes):
        for t in writes:
            t.w = ev
            t.r = []
        for t in reads:
            if t not in writes:
                t.r = [x for x in t.r if x[0] != ev[0]] + [ev]

    def op(self, eng, fn, reads=(), writes=(), inc=True):
        key = "E_" + eng
        self._deps(eng, reads, writes)
        ins = fn(self.eng[eng])
        if inc:
            self.cnt[key] += 1
            ev = (key, self.cnt[key])
            ins.then_inc(self.sems[key], 1)
        else:
            ev = (key, self.cnt[key] + 1)
        self._mark(ev, reads, writes)

    def dma(self, eng, dsem, out_ap, in_ap, reads=(), writes=(), **kw):
        self._deps(eng, reads, writes)
        self.cnt[dsem] += 16
        ev = (dsem, self.cnt[dsem])
        self.eng[eng].dma_start(out=out_ap, in_=in_ap, **kw).then_inc(self.sems[dsem], 16)
        self._mark(ev, reads, writes)

    def barrier(self):
        for eng in ENGS:
            e = self.eng[eng]
            for k, v in self.cnt.items():
                if v > 0 and self.waited[eng].get(k, 0) < v:
                    if k == "E_" + eng:
                        continue
                    self.waited[eng][k] = v
                    e.wait_ge(self.sems[k], v)


D = 2048
NTOK = 2048
SBK = 512
NSB = NTOK // SBK
NTT = SBK // 128
PBK = 256
EPS = 1e-6
NEG = -1.0e30
OHS = 12.0
OH_HOT = 1.125

O_NMW, O_NFW, O_LCW, O_LCB, O_LBA, O_LBX, O_LAM = 0, 16, 32, 64, 72, 80, 88
O_SCW, O_SCB, O_SNW, O_DTB, O_ALOG, O_SD = 96, 144, 156, 164, 180, 196
NSP = 212


def bc(ap, axis, shape):
    return ap.unsqueeze(axis).broadcast_to(list(shape))


def build_nc(nsb=NSB, do_peer=True, dbg=False):
    nc = bass.Bass("TRN2", target_bir_lowering=False)
    dt_ = nc.dram_tensor
    x_d = dt_("x", [NTOK, D], F32, kind="ExternalInput")
    sp_d = dt_("sp", [128, NSP], F32, kind="ExternalInput")
    nfin_d = dt_("nfin", [128, D], F32, kind="ExternalInput")
    win_d = dt_("win", [28, 128, 16 * 128], F32, kind="ExternalInput")
    wz_d = dt_("wz", [2, 128, 16 * 512], F32, kind="ExternalInput")
    wdt_d = dt_("wdt", [128, 16 * 16], F32, kind="ExternalInput")
    wabd_d = dt_("wabd", [128, 8 * 128], F32, kind="ExternalInput")
    wxbd_d = dt_("wxbd", [128, 8 * 128], F32, kind="ExternalInput")
    wout_d = dt_("wout", [4, 128, 16 * 512], F32, kind="ExternalInput")
    wq_d = dt_("wq", [16, 128, 16 * 128], F32, kind="ExternalInput")
    skt_d = dt_("skt", [128, 16 * 128], F32, kind="ExternalInput")
    ut_d = dt_("ut", [128, 128, 16 * 128], F32, kind="ExternalInput")
    v_d = dt_("v", [128, 128, D], F32, kind="ExternalInput")
    out_d = dt_("out", [NTOK, D], F32, kind="ExternalOutput")
    gd_d = dt_("gd", [128, 128, NTOK // 128, 128], BF16, kind="Internal")
    x2_d = dt_("x2s", [NTOK, D], F32, kind="Internal")
    h2_d = dt_("h2s", [128, 16, NTOK], BF16, kind="Internal")
    if dbg:
        dbg_d = dt_("dbg", [128, 8192], F32, kind="ExternalOutput")

    with ExitStack() as es:
        fw = FW(nc, es)
        outT = T("out", out_d)
        gdT = T("gd", gd_d)
        gdTs = [T("gd%d" % i, gd_d) for i in range(4)]
        x2Ts = [T("x2s%d" % i, x2_d) for i in range(16)]
        outTs = [T("out%d" % i, out_d) for i in range(8)]
        h2T_ = T("h2s", h2_d)
        dbg_stage = [fw.sb("dbgst%d" % i, [128, n]) for i, n in enumerate([512, 512, 512, 512, 128, 128, 128])] if dbg else []
        psum1 = ExitStack()
        p1 = ExitStack()
        _sb = fw.sb
        fw.sb = lambda name, shape, dtype=F32, es=None: _sb(name, shape, dtype, es=(es or p1))
        PF = [fw.ps("pf%d" % i, [128, 512], F32, es=psum1) for i in range(6)]
        PB = [fw.ps("pb%d" % i, [128, 1024], BF16, es=psum1) for i in range(2)]
        spk = fw.sb("spk", [128, NSP])
        identf = fw.sb("identf", [128, 128])
        identb = fw.sb("identb", [128, 128], BF16)
        triU = fw.sb("triU", [128, 128])
        triS = fw.sb("triS", [128, 128])
        ones = fw.sb("ones", [128, 128])
        iota128 = fw.sb("iota128", [128, 128])
        wabd = fw.sb("wabd", [128, 8, 128], BF16)
        wxbd = fw.sb("wxbd", [128, 8, 128], BF16)
        wdt = fw.sb("wdt", [128, 16, 16], BF16)
        skt = fw.sb("skt", [128, 16, 128], BF16)
        clam = fw.sb("clam", [128, 8])
        clam2 = fw.sb("clam2", [128, 8])
        arow = fw.sb("arow", [128, 16])
        halo = fw.sb("halo", [128, 20, 3])
        lstate = fw.sb("lstate", [128, 8])
        sstate = fw.sb("sstate", [128, 1024])
        sstate_b = fw.sb("sstate_b", [128, 1024], BF16)
        xt = [fw.sb("xt%d" % i, [128, D]) for i in range(NTT)]
        hT = fw.sb("hT", [128, 16, SBK], BF16)
        IT = fw.sb("IT", [128, SBK])
        JT = fw.sb("JT", [128, SBK])
        GT = fw.sb("GT", [128, SBK])
        NJT = fw.sb("NJT", [128, SBK])
        ssq = fw.sb("ssq", [128, 8])
        rstd = fw.sb("rstd", [128, 8])
        hb = fw.sb("hb", [128, D], BF16)

        dc_ = fw.dmasem("const")
        dcg = fw.dmasem("constg")
        dx = [fw.dmasem("x%d" % i) for i in range(NTT)]
        dout = fw.dmasem("out")
        dwb = [fw.dmasem("wb%d" % i) for i in range(3)]
        dwz = fw.dmasem("wz")
        dwo = [fw.dmasem("wo%d" % i) for i in range(2)]
        dwq = [fw.dmasem("wq%d" % i) for i in range(2)]
        du = [fw.dmasem("u%d" % i) for i in range(2)]
        dv = [fw.dmasem("v%d" % i) for i in range(2)]
        dgo = [fw.dmasem("gout%d" % i) for i in range(4)]
        dgi = [fw.dmasem("gin%d" % i) for i in range(2)]
        dx2 = [fw.dmasem("x2o%d" % i) for i in range(NTT)]
        dh2 = fw.dmasem("h2o")
        dx2i = [fw.dmasem("x2i%d" % i) for i in range(8)]
        dh2i = fw.dmasem("h2i")
        douts = [fw.dmasem("out%d" % i) for i in range(8)]

        dcs = [fw.dmasem("cst%d" % i) for i in range(4)]
        fw.dma(SP, dc_, spk[:], sp_d.ap(), writes=[spk])
        dnf = fw.dmasem("nfin")
        fw.dma(G, dcs[0], wabd[:].rearrange("p a b -> p (a b)"), wabd_d.ap(), writes=[wabd])
        fw.dma(G, dcs[1], wxbd[:].rearrange("p a b -> p (a b)"), wxbd_d.ap(), writes=[wxbd])
        fw.dma(G, dcs[2], wdt[:].rearrange("p a b -> p (a b)"), wdt_d.ap(), writes=[wdt])
        fw.dma(G, dcs[3], skt[:].rearrange("p a b -> p (a b)"), skt_d.ap(), writes=[skt])

        fw.op(G, lambda e: e.memset(ones[:], 1.0), writes=[ones])
        fw.op(G, lambda e: e.memset(identf[:], 1.0), writes=[identf])
        fw.op(G, lambda e: e.affine_select(out=identf[:], in_=identf[:], pattern=[[1, 128]], compare_op=ALU.is_equal,
                                           fill=0.0, base=0, channel_multiplier=-1), reads=[identf], writes=[identf])
        fw.op(V, lambda e: e.tensor_copy(identb[:], identf[:]), reads=[identf], writes=[identb])
        fw.op(G, lambda e: e.affine_select(out=triU[:], in_=ones[:], pattern=[[1, 128]], compare_op=ALU.is_ge,
                                           fill=0.0, base=0, channel_multiplier=-1), reads=[ones], writes=[triU])
        fw.op(G, lambda e: e.affine_select(out=triS[:], in_=ones[:], pattern=[[-1, 128]], compare_op=ALU.is_gt,
                                           fill=0.0, base=0, channel_multiplier=1), reads=[ones], writes=[triS])
        fw.op(G, lambda e: e.iota(iota128[:], pattern=[[1, 128]], base=0, channel_multiplier=0,
                                  allow_small_or_imprecise_dtypes=True), writes=[iota128])
        fw.op(V, lambda e: e.memset(halo[:], 0.0), writes=[halo])
        fw.op(V, lambda e: e.memset(lstate[:], 0.0), writes=[lstate])
        fw.op(V, lambda e: e.memset(sstate[:], 0.0), writes=[sstate])
        fw.op(V, lambda e: e.memset(sstate_b[:], 0.0), writes=[sstate_b])

        ee = fw.sb("ee", [128, 8])
        pp = fw.sb("pp", [128, 8])
        lam = spk[:, O_LAM:O_LAM + 8]
        fw.op(S, lambda e: e.activation(out=ee[:], in_=lam, func=AF.Exp, scale=-1.0), reads=[spk], writes=[ee])
        fw.op(V, lambda e: e.tensor_scalar(out=pp[:], in0=ee[:], scalar1=-0.25, scalar2=1.0 / 3.0, op0=ALU.mult, op1=ALU.add), reads=[ee], writes=[pp])
        fw.op(V, lambda e: e.tensor_tensor(out=pp[:], in0=pp[:], in1=ee[:], op=ALU.mult), reads=[pp, ee], writes=[pp])
        fw.op(V, lambda e: e.tensor_scalar(out=pp[:], in0=pp[:], scalar1=-0.5, scalar2=None, op0=ALU.add), reads=[pp], writes=[pp])
        fw.op(V, lambda e: e.tensor_tensor(out=pp[:], in0=pp[:], in1=ee[:], op=ALU.mult), reads=[pp, ee], writes=[pp])
        fw.op(V, lambda e: e.tensor_scalar(out=pp[:], in0=pp[:], scalar1=1.0, scalar2=None, op0=ALU.add), reads=[pp], writes=[pp])
        fw.op(V, lambda e: e.tensor_tensor(out=pp[:], in0=pp[:], in1=ee[:], op=ALU.mult), reads=[pp, ee], writes=[pp])
        fw.op(V, lambda e: e.tensor_scalar(out=clam[:], in0=pp[:], scalar1=-8.0, scalar2=None, op0=ALU.mult), reads=[pp], writes=[clam])
        fw.op(V, lambda e: e.tensor_scalar(out=clam2[:], in0=pp[:], scalar1=-16.0, scalar2=None, op0=ALU.mult), reads=[pp], writes=[clam2])
        fw.op(S, lambda e: e.activation(out=arow[:], in_=spk[:, O_ALOG:O_ALOG + 16], func=AF.Exp), reads=[spk], writes=[arow])
        fw.op(V, lambda e: e.tensor_scalar(out=arow[:], in0=arow[:], scalar1=-1.0, scalar2=None, op0=ALU.mult), reads=[arow], writes=[arow])

        dbg_col = [0]

        def tap(src_t, ap, n):
            if not dbg:
                return
            c0 = dbg_col[0]
            stage = T("stg", dbg_stage.pop(0).t[:, 0:n])
            fw.op(V, lambda e: e.tensor_copy(stage.t, ap), reads=[src_t], writes=[stage])
            ds_ = fw.dmasem("dbg%d" % c0)
            dT = T("dbgd%d" % c0, dbg_d)
            fw.dma(SP, ds_, dbg_d.ap()[:, c0:c0 + n], stage.t, reads=[stage], writes=[dT])
            taps.append(dT)
            dbg_col[0] += n
        taps = []

        def rms_and_transpose(tt, wcol_off, first_load):
            x_t = xt[tt]
            fw.op(S, lambda e: e.activation(out=hb[:], in_=x_t[:], func=AF.Square, accum_out=ssq[:, tt:tt + 1]),
                  reads=[x_t], writes=[hb, ssq])
            fw.op(V, lambda e: e.tensor_scalar(out=rstd[:, tt:tt + 1], in0=ssq[:, tt:tt + 1], scalar1=1.0 / D, scalar2=EPS,
                                               op0=ALU.mult, op1=ALU.add), reads=[ssq], writes=[rstd])
            fw.op(S, lambda e: e.activation(out=rstd[:, tt:tt + 1], in_=rstd[:, tt:tt + 1], func=AF.Sqrt), reads=[rstd], writes=[rstd])
            fw.op(V, lambda e: e.reciprocal(out=rstd[:, tt:tt + 1], in_=rstd[:, tt:tt + 1]), reads=[rstd], writes=[rstd])
            fw.op(V, lambda e: e.tensor_scalar(out=hb[:], in0=x_t[:], scalar1=rstd[:, tt:tt + 1], scalar2=None, op0=ALU.mult),
                  reads=[x_t, rstd], writes=[hb])
            for half in range(2):
                pb = PB[half]
                for q in range(8):
                    dcn = half * 8 + q
                    fw.op(PE, lambda e, q=q, dcn=dcn, pb=pb: e.transpose(out=pb[:, q * 128:(q + 1) * 128], in_=hb[:, dcn * 128:(dcn + 1) * 128],
                                                                      identity=identb[:]), reads=[hb, identb], writes=[pb], inc=(q == 7))
                wv = spk[:, wcol_off + half * 8: wcol_off + half * 8 + 8]
                fw.op(V, lambda e, pb=pb, wv=wv, half=half: e.tensor_tensor(
                    out=hT[:, half * 8:(half + 1) * 8, tt * 128:(tt + 1) * 128],
                    in0=pb[:].rearrange("p (a b) -> p a b", b=128), in1=bc(wv, 2, [128, 8, 128]), op=ALU.mult),
                    reads=[pb, spk], writes=[hT])

        for sbi in range(nsb):
            t0 = sbi * SBK
            for tt in range(NTT):
                fw.dma(SP, dx[tt], xt[tt][:], x_d.ap()[t0 + tt * 128: t0 + (tt + 1) * 128, :], writes=[xt[tt]])
            for tt in range(NTT):
                rms_and_transpose(tt, O_NMW, True)
            if sbi == 0:
                tap(hT, hT[:, 0, :], 512)

            bes = ExitStack()
            mixT = fw.sb("mixT", [128, 16, SBK], BF16, es=bes)
            with ExitStack() as mes:
                xsT = fw.sb("xsT", [128, 8, SBK], es=mes)
                BT = fw.sb("BT", [128, 2, SBK], BF16, es=mes)
                CT = fw.sb("CT", [128, 2, SBK], BF16, es=mes)
                zs = [fw.sb("zs%d" % i, [128, 1024], es=mes) for i in range(NTT)]
                dtk = fw.sb("dtk", [128, NTT, 16], es=mes)
                adt = fw.sb("adt", [128, NTT, 16], es=mes)
                mes1 = ExitStack()
                wbuf = [fw.sb("wbuf%d" % i, [128, 16, 128], BF16, es=mes1) for i in range(3)]
                xraw = [fw.sb("xraw%d" % i, [128, SBK + 3], es=mes1) for i in range(2)]
                xl = fw.sb("xl", [128, SBK], es=mes1)
                xlb = fw.sb("xlb", [128, SBK], BF16, es=mes1)
                rg = fw.sb("rg", [128, SBK], es=mes1)
                ig = fw.sb("ig", [128, SBK], es=mes1)
                av = fw.sb("av", [128, SBK], es=mes1)
                bv = fw.sb("bv", [128, SBK], es=mes1)
                hs = fw.sb("hs", [128, SBK], es=mes1)
                gg = fw.sb("gg", [128, SBK], es=mes1)
                wzb = fw.sb("wzb", [128, 16, 512], BF16, es=mes1)

                wq_i = [0]

                def load_w(blk):
                    i = wq_i[0] % 3
                    wq_i[0] += 1
                    wb = wbuf[i]
                    fw.dma(G, dwb[i], wb[:].rearrange("p a b -> p (a b)"), win_d.ap()[blk], writes=[wb])
                    return wb

                def proj_block(wb, pf):
                    for dcn in range(16):
                        fw.op(PE, lambda e, dcn=dcn: e.matmul(pf[:], lhsT=wb[:, dcn, :], rhs=hT[:, dcn, :], start=(dcn == 0), stop=(dcn == 15)),
                              reads=[wb, hT], writes=[pf], inc=(dcn == 15))

                def conv_block(pf, hidx, wcol, bcol, xr, out_ap, out_t, func):
                    fw.op(V, lambda e: e.tensor_copy(xr[:, 0:3], halo[:, hidx, :]), reads=[halo], writes=[xr])
                    fw.op(S, lambda e: e.copy(out=xr[:, 3:SBK + 3], in_=pf[:]), reads=[pf], writes=[xr])
                    fw.op(V, lambda e: e.tensor_copy(halo[:, hidx, :], xr[:, SBK:SBK + 3]), reads=[xr], writes=[halo])
                    w = lambda k: spk[:, wcol + k: wcol + k + 1]
                    fw.op(V, lambda e: e.tensor_scalar(out=xl[:], in0=xr[:, 3:SBK + 3], scalar1=w(3), scalar2=spk[:, bcol:bcol + 1],
                                                       op0=ALU.mult, op1=ALU.add), reads=[xr, spk], writes=[xl])
                    for k in (2, 1, 0):
                        fw.op(V, lambda e, k=k: e.scalar_tensor_tensor(out=xl[:], in0=xr[:, k:SBK + k], scalar=w(k), in1=xl[:],
                                                                       op0=ALU.mult, op1=ALU.add), reads=[xr, spk, xl], writes=[xl])
                    if func is not None:
                        fw.op(S, lambda e: e.activation(out=out_ap, in_=xl[:], func=func), reads=[xl], writes=[out_t])

                pfi = [0]

                def next_pf():
                    p = PF[pfi[0] % 3]
                    pfi[0] += 1
                    return p

                for k in range(8):
                    wbx = load_w(k)
                    wbg = load_w(8 + k)
                    pfx = next_pf()
                    proj_block(wbx, pfx)
                    xr = xraw[k % 2]
                    conv_block(pfx, k, O_LCW + 4 * k, O_LCB + k, xr, None, None, None)
                    fw.op(V, lambda e: e.tensor_copy(xlb[:], xl[:]), reads=[xl], writes=[xlb])
                    fw.op(PE, lambda e: e.matmul(PF[3][:], lhsT=wabd[:, k, :], rhs=xlb[:], start=True, stop=True), reads=[wabd, xlb], writes=[PF[3]])
                    fw.op(PE, lambda e: e.matmul(PF[4][:], lhsT=wxbd[:, k, :], rhs=xlb[:], start=True, stop=True), reads=[wxbd, xlb], writes=[PF[4]])
                    fw.op(S, lambda e: e.activation(out=rg[:], in_=PF[3][:], func=AF.Sigmoid, bias=spk[:, O_LBA + k:O_LBA + k + 1]),
                          reads=[PF[3], spk], writes=[rg])
                    fw.op(S, lambda e: e.activation(out=ig[:], in_=PF[4][:], func=AF.Sigmoid, bias=spk[:, O_LBX + k:O_LBX + k + 1]),
                          reads=[PF[4], spk], writes=[ig])
                    fw.op(S, lambda e: e.activation(out=av[:], in_=rg[:], func=AF.Exp, scale=clam[:, k:k + 1]), reads=[rg, clam], writes=[av])
                    fw.op(S, lambda e: e.activation(out=bv[:], in_=rg[:], func=AF.Exp, scale=clam2[:, k:k + 1]), reads=[rg, clam2], writes=[bv])
                    fw.op(V, lambda e: e.tensor_scalar(out=bv[:], in0=bv[:], scalar1=-1.0, scalar2=1.0, op0=ALU.mult, op1=ALU.add), reads=[bv], writes=[bv])
                    fw.op(S, lambda e: e.activation(out=bv[:], in_=bv[:], func=AF.Sqrt), reads=[bv], writes=[bv])
                    fw.op(V, lambda e: e.tensor_tensor(out=ig[:], in0=ig[:], in1=xl[:], op=ALU.mult), reads=[ig, xl], writes=[ig])
                    fw.op(V, lambda e: e.tensor_tensor(out=bv[:], in0=bv[:], in1=ig[:], op=ALU.mult), reads=[bv, ig], writes=[bv])
                    fw.op(V, lambda e: e.tensor_tensor_scan(out=hs[:], data0=av[:], data1=bv[:], initial=lstate[:, k:k + 1],
                                                            op0=ALU.mult, op1=ALU.add), reads=[av, bv, lstate], writes=[hs])
                    fw.op(V, lambda e: e.tensor_copy(lstate[:, k:k + 1], hs[:, SBK - 1:SBK]), reads=[hs], writes=[lstate])
                    pfg = next_pf()
                    proj_block(wbg, pfg)
                    fw.op(S, lambda e: e.activation(out=gg[:], in_=pfg[:], func=AF.Gelu_apprx_tanh), reads=[pfg], writes=[gg])
                    fw.op(V, lambda e: e.tensor_tensor(out=mixT[:, k, :], in0=hs[:], in1=gg[:], op=ALU.mult), reads=[hs, gg], writes=[mixT])
                if sbi == 0:
                    tap(mixT, mixT[:, 0, :], 512)

                for k in range(12):
                    wb = load_w(16 + k)
                    pf = next_pf()
                    proj_block(wb, pf)
                    xr = xraw[k % 2]
                    if k < 8:
                        conv_block(pf, 8 + k, O_SCW + 4 * k, O_SCB + k, xr, xsT[:, k, :], xsT, AF.Silu)
                    elif k < 10:
                        conv_block(pf, 8 + k, O_SCW + 4 * k, O_SCB + k, xr, BT[:, k - 8, :], BT, AF.Silu)
                    else:
                        conv_block(pf, 8 + k, O_SCW + 4 * k, O_SCB + k, xr, CT[:, k - 10, :], CT, AF.Silu)
                for half in range(2):
                    for a in range(4):
                        fw.dma(G, dwz, wzb[:, a * 4:(a + 1) * 4, :].rearrange("p a b -> p (a b)"),
                               wz_d.ap()[half][:, a * 2048:(a + 1) * 2048], writes=[wzb])
                    for tt in range(NTT):
                        pf = next_pf()
                        for dcn in range(16):
                            fw.op(PE, lambda e, dcn=dcn: e.matmul(pf[:], lhsT=hT[:, dcn, tt * 128:(tt + 1) * 128], rhs=wzb[:, dcn, :],
                                                                  start=(dcn == 0), stop=(dcn == 15)), reads=[hT, wzb], writes=[pf], inc=(dcn == 15))
                        fw.op(S, lambda e: e.activation(out=zs[tt][:, half * 512:(half + 1) * 512], in_=pf[:], func=AF.Silu), reads=[pf], writes=[zs[tt]])
                for tt in range(NTT):
                    for dcn in range(16):
                        fw.op(PE, lambda e, dcn=dcn: e.matmul(PF[5][:, 0:16], lhsT=hT[:, dcn, tt * 128:(tt + 1) * 128], rhs=wdt[:, dcn, :],
                                                              start=(dcn == 0), stop=(dcn == 15)), reads=[hT, wdt], writes=[PF[5]], inc=(dcn == 15))
                    fw.op(V, lambda e: e.tensor_tensor(out=dtk[:, tt, :], in0=PF[5][:, 0:16], in1=spk[:, O_DTB:O_DTB + 16], op=ALU.add),
                          reads=[PF[5], spk], writes=[dtk])
                fw.op(S, lambda e: e.activation(out=dtk[:], in_=dtk[:], func=AF.Exp), reads=[dtk], writes=[dtk])
                fw.op(S, lambda e: e.activation(out=dtk[:], in_=dtk[:], func=AF.Ln, bias=1.0), reads=[dtk], writes=[dtk])
                fw.op(V, lambda e: e.tensor_tensor(out=adt[:], in0=dtk[:], in1=bc(arow[:], 1, [128, NTT, 16]), op=ALU.mult),
                      reads=[dtk, arow], writes=[adt])

                fw.barrier()
                mes1.close()
                mes = ExitStack()
                csx = fw.sb("csx", [128, 16], es=mes)
                dcy = fw.sb("dcy", [128, 16], es=mes)
                dtd = fw.sb("dtd", [128, 16], es=mes)
                etot = fw.sb("etot", [128, 16], es=mes)
                Rm = fw.sb("Rm", [128, 16, 128], es=mes)
                Eb = [fw.sb("Eb%d" % i, [128, 4, 128], es=mes) for i in range(2)]
                cbm = fw.sb("cbm", [128, 2, 128], es=mes)
                scT = fw.sb("scT", [128, 16, 128], BF16, es=mes)
                xtok = fw.sb("xtok", [128, 16, 64], es=mes)
                xc = fw.sb("xc", [128, 16, 64], BF16, es=mes)
                xcd = fw.sb("xcd", [128, 16, 64], BF16, es=mes)
                btok = fw.sb("btok", [128, 2, 128], BF16, es=mes)
                yv = fw.sb("yv", [128, 16, 64], es=mes)
                y2 = fw.sb("y2", [128, 16, 64], es=mes)
                ynb = fw.sb("ynb", [128, 1024], BF16, es=mes)
                ss2 = fw.sb("ss2", [128, 1], es=mes)
                rs2 = fw.sb("rs2", [128, 1], es=mes)
                for c in range(NTT):
                    cs_ = slice(c * 128, (c + 1) * 128)
                    fw.op(PE, lambda e: e.matmul(PF[5][:, 0:16], lhsT=triU[:], rhs=adt[:, c, :], start=True, stop=True), reads=[triU, adt], writes=[PF[5]], inc=False)
                    fw.op(PE, lambda e: e.matmul(PF[5][:, 16:32], lhsT=ones[:], rhs=adt[:, c, :], start=True, stop=True), reads=[ones, adt], writes=[PF[5]])
                    fw.op(S, lambda e: e.activation(out=csx[:], in_=PF[5][:, 0:16], func=AF.Exp), reads=[PF[5]], writes=[csx])
                    fw.op(V, lambda e: e.tensor_tensor(out=dcy[:], in0=PF[5][:, 16:32], in1=csx[:], op=ALU.subtract), reads=[PF[5], csx], writes=[dcy]) if False else None
                    fw.op(V, lambda e: e.tensor_copy(dtd[:], PF[5][:, 0:16]), reads=[PF[5]], writes=[dtd])
                    fw.op(V, lambda e: e.tensor_tensor(out=dcy[:], in0=PF[5][:, 16:32], in1=dtd[:], op=ALU.subtract), reads=[PF[5], dtd], writes=[dcy])
                    fw.op(S, lambda e: e.activation(out=dcy[:], in_=dcy[:], func=AF.Exp), reads=[dcy], writes=[dcy])
                    fw.op(S, lambda e: e.activation(out=etot[:], in_=PF[5][:, 16:32], func=AF.Exp), reads=[PF[5]], writes=[etot])
                    fw.op(V, lambda e: e.tensor_tensor(out=dtd[:], in0=dtk[:, c, :], in1=dcy[:], op=ALU.mult), reads=[dtk, dcy], writes=[dtd])
                    for g in range(2):
                        fw.op(PE, lambda e, g=g: e.matmul(PF[2][:, g * 128:(g + 1) * 128], lhsT=BT[:, g, cs_], rhs=CT[:, g, cs_], start=True, stop=True),
                              reads=[BT, CT], writes=[PF[2]], inc=(g == 1))
                    fw.op(V, lambda e: e.tensor_tensor(out=cbm[:], in0=PF[2][:, 0:256].rearrange("p (a b) -> p a b", b=128),
                                                       in1=bc(triU[:], 1, [128, 2, 128]), op=ALU.mult), reads=[PF[2], triU], writes=[cbm])
                    for k in range(8):
                        pf = PF[3 + k // 4]
                        fw.op(PE, lambda e, k=k, pf=pf: e.transpose(out=pf[:, (k % 4) * 128:(k % 4 + 1) * 128], in_=xsT[:, k, cs_], identity=identf[:]),
                              reads=[xsT, identf], writes=[pf], inc=(k % 4 == 3))
                    for hh in range(2):
                        fw.op(S, lambda e, hh=hh: e.copy(out=xtok[:, hh * 8:(hh + 1) * 8, :].rearrange("p a b -> p (a b)"), in_=PF[3 + hh][:]),
                              reads=[PF[3 + hh]], writes=[xtok])
                    fw.op(V, lambda e: e.tensor_tensor(out=xc[:], in0=xtok[:], in1=bc(dtk[:, c, :], 2, [128, 16, 64]), op=ALU.mult), reads=[xtok, dtk], writes=[xc])
                    fw.op(V, lambda e: e.tensor_tensor(out=xcd[:], in0=xtok[:], in1=bc(dtd[:], 2, [128, 16, 64]), op=ALU.mult), reads=[xtok, dtd], writes=[xcd])
                    fw.op(V, lambda e: e.tensor_tensor(out=Rm[:], in0=bc(triU[:], 1, [128, 16, 128]), in1=bc(adt[:, c, :], 2, [128, 16, 128]), op=ALU.mult),
                          reads=[triU, adt], writes=[Rm])
                    for q in range(4):
                        pf = PF[q % 2]
                        fw.op(PE, lambda e, q=q, pf=pf: e.matmul(pf[:], lhsT=triS[:], rhs=Rm[:, q * 4:(q + 1) * 4, :].rearrange("p a b -> p (a b)"), start=True, stop=True),
                              reads=[triS, Rm], writes=[pf])
                        eb = Eb[q % 2]
                        fw.op(S, lambda e, pf=pf, eb=eb: e.activation(out=eb[:].rearrange("p a b -> p (a b)"), in_=pf[:], func=AF.Exp), reads=[pf], writes=[eb])
                        fw.op(V, lambda e, q=q, eb=eb: e.tensor_tensor(out=scT[:, q * 4:(q + 1) * 4, :], in0=eb[:],
                                                                    in1=bc(cbm[:, q // 2, :], 1, [128, 4, 128]), op=ALU.mult), reads=[eb, cbm], writes=[scT])
                    for h in range(16):
                        pf = PF[3 + h // 8]
                        fw.op(PE, lambda e, h=h, pf=pf: e.matmul(pf[:, (h % 8) * 64:(h % 8 + 1) * 64], lhsT=scT[:, h, :], rhs=xc[:, h, :], start=True, stop=True),
                              reads=[scT, xc], writes=[pf], inc=(h % 8 == 7))
                    for g in range(2):
                        fw.op(PE, lambda e, g=g: e.matmul(PF[g][:], lhsT=CT[:, g, cs_], rhs=sstate_b[:, g * 512:(g + 1) * 512], start=True, stop=True),
                              reads=[CT, sstate_b], writes=[PF[g]])
                    for g in range(2):
                        hsl = slice(g * 8, (g + 1) * 8)
                        fw.op(V, lambda e, g=g, hsl=hsl: e.tensor_tensor(out=yv[:, hsl, :], in0=PF[g][:].rearrange("p (a b) -> p a b", b=64),
                                                                      in1=bc(csx[:, hsl], 2, [128, 8, 64]), op=ALU.mult), reads=[PF[g], csx], writes=[yv])
                        fw.op(V, lambda e, g=g, hsl=hsl: e.tensor_tensor(out=yv[:, hsl, :], in0=PF[3 + g][:].rearrange("p (a b) -> p a b", b=64),
                                                                      in1=yv[:, hsl, :], op=ALU.add), reads=[PF[3 + g], yv], writes=[yv])
                    fw.op(V, lambda e: e.tensor_tensor(out=y2[:], in0=xtok[:], in1=bc(spk[:, O_SD:O_SD + 16], 2, [128, 16, 64]), op=ALU.mult),
                          reads=[xtok, spk], writes=[y2])
                    fw.op(V, lambda e: e.tensor_tensor(out=yv[:], in0=yv[:], in1=y2[:], op=ALU.add), reads=[yv, y2], writes=[yv])
                    for g in range(2):
                        fw.op(PE, lambda e, g=g: e.transpose(out=PB[0][:, g * 128:(g + 1) * 128], in_=BT[:, g, cs_], identity=identb[:]),
                              reads=[BT, identb], writes=[PB[0]], inc=(g == 1))
                    fw.op(S, lambda e: e.copy(out=btok[:].rearrange("p a b -> p (a b)"), in_=PB[0][:, 0:256]), reads=[PB[0]], writes=[btok])
                    for g in range(2):
                        fw.op(PE, lambda e, g=g: e.matmul(PF[g][:], lhsT=btok[:, g, :], rhs=xcd[:, g * 8:(g + 1) * 8, :].rearrange("p a b -> p (a b)"), start=True, stop=True),
                              reads=[btok, xcd], writes=[PF[g]])
                    fw.op(V, lambda e: e.tensor_tensor(out=sstate[:].rearrange("p (a b) -> p a b", b=64), in0=sstate[:].rearrange("p (a b) -> p a b", b=64),
                                                       in1=bc(etot[:], 2, [128, 16, 64]), op=ALU.mult), reads=[sstate, etot], writes=[sstate])
                    for g in range(2):
                        fw.op(V, lambda e, g=g: e.tensor_tensor(out=sstate[:, g * 512:(g + 1) * 512], in0=PF[g][:], in1=sstate[:, g * 512:(g + 1) * 512], op=ALU.add),
                              reads=[PF[g], sstate], writes=[sstate])
                    fw.op(S, lambda e: e.copy(out=sstate_b[:], in_=sstate[:]), reads=[sstate], writes=[sstate_b])
                    yflat = yv[:].rearrange("p a b -> p (a b)")
                    fw.op(V, lambda e: e.tensor_tensor(out=yflat, in0=yflat, in1=zs[c][:], op=ALU.mult), reads=[yv, zs[c]], writes=[yv])
                    fw.op(S, lambda e: e.activation(out=ynb[:], in_=yflat, func=AF.Square, accum_out=ss2[:]), reads=[yv], writes=[ynb, ss2])
                    fw.op(V, lambda e: e.tensor_scalar(out=rs2[:], in0=ss2[:], scalar1=1.0 / 1024, scalar2=EPS, op0=ALU.mult, op1=ALU.add), reads=[ss2], writes=[rs2])
                    fw.op(S, lambda e: e.activation(out=rs2[:], in_=rs2[:], func=AF.Sqrt), reads=[rs2], writes=[rs2])
                    fw.op(V, lambda e: e.reciprocal(out=rs2[:], in_=rs2[:]), reads=[rs2], writes=[rs2])
                    fw.op(V, lambda e: e.tensor_scalar(out=ynb[:], in0=yflat, scalar1=rs2[:], scalar2=None, op0=ALU.mult), reads=[yv, rs2], writes=[ynb])
                    for k in range(8):
                        fw.op(PE, lambda e, k=k: e.transpose(out=PB[1][:, k * 128:(k + 1) * 128], in_=ynb[:, k * 128:(k + 1) * 128], identity=identb[:]),
                              reads=[ynb, identb], writes=[PB[1]], inc=(k == 7))
                    fw.op(V, lambda e: e.tensor_tensor(out=mixT[:, 8:16, cs_], in0=PB[1][:].rearrange("p (a b) -> p a b", b=128),
                                                       in1=bc(spk[:, O_SNW:O_SNW + 8], 2, [128, 8, 128]), op=ALU.mult), reads=[PB[1], spk], writes=[mixT])
                if sbi == 0:
                    tap(mixT, mixT[:, 8, :], 512)

                fw.barrier()
                mes.close()
                mes = ExitStack()
                wob = [fw.sb("wob%d" % i, [128, 16, 512], BF16, es=mes) for i in range(2)]
                for nb in range(4):
                    wo = wob[nb % 2]
                    for a in range(4):
                        fw.dma(G, dwo[nb % 2], wo[:, a * 4:(a + 1) * 4, :].rearrange("p a b -> p (a b)"),
                               wout_d.ap()[nb][:, a * 2048:(a + 1) * 2048], writes=[wo])
                    for tt in range(NTT):
                        pf = next_pf()
                        for cc in range(16):
                            fw.op(PE, lambda e, cc=cc: e.matmul(pf[:], lhsT=mixT[:, cc, tt * 128:(tt + 1) * 128], rhs=wo[:, cc, :],
                                                                start=(cc == 0), stop=(cc == 15)), reads=[mixT, wo], writes=[pf], inc=(cc == 15))
                        fw.op(V, lambda e: e.tensor_tensor(out=xt[tt][:, nb * 512:(nb + 1) * 512], in0=pf[:], in1=xt[tt][:, nb * 512:(nb + 1) * 512], op=ALU.add),
                              reads=[pf, xt[tt]], writes=[xt[tt]])
                if sbi == 0:
                    tap(xt[0], xt[0][:, 0:512], 512)
                fw.barrier()
                mes.close()
            for tt in range(NTT):
                fw.dma(SP, dx2[tt], x2_d.ap()[t0 + tt * 128: t0 + (tt + 1) * 128, :], xt[tt][:], reads=[xt[tt]], writes=[x2Ts[sbi * NTT + tt]])
            if do_peer:
                for tt in range(NTT):
                    rms_and_transpose(tt, O_NFW, False)
                fw.dma(SP, dh2, h2_d.ap()[:, :, t0:t0 + SBK], hT[:], reads=[hT], writes=[h2T_])
                with ExitStack() as pes:
                    peer_route_gbuild(fw, pes, sbi, locals())
                    fw.barrier()
                bes.close()
            else:
                bes.close()
        fw.barrier()
        p1.close()
        psum1.close()
        fw.sb = _sb
        if True:
            TB2 = 1024
            NT2 = TB2 // 128
            JG = 4
            PA = [fw.ps("pa%d" % i, [128, 512], F32) for i in range(4)]
            PO = [fw.ps("po%d" % i, [128, 512], F32) for i in range(3)]
            PG = fw.ps("pg", [128, 1024], BF16)
            identb2 = fw.sb("identb2", [128, 128], BF16)
            identf2 = fw.sb("identf2", [128, 128])
            fw.op(G, lambda e: e.memset(identf2[:], 1.0), writes=[identf2])
            fw.op(G, lambda e: e.affine_select(out=identf2[:], in_=identf2[:], pattern=[[1, 128]], compare_op=ALU.is_equal,
                                               fill=0.0, base=0, channel_multiplier=-1), reads=[identf2], writes=[identf2])
            fw.op(V, lambda e: e.tensor_copy(identb2[:], identf2[:]), reads=[identf2], writes=[identb2])
            acc = [fw.sb("acc%d" % i, [128, D]) for i in range(NT2)]
            h2 = fw.sb("h2", [128, 16, TB2], BF16)
            vb = [fw.sb("vb%d" % i, [128, JG, D], BF16) for i in range(2)]
            ub = [fw.sb("ub%d" % i, [128, 16 * 128], BF16) for i in range(2)]
            gb = [fw.sb("gb%d" % i, [128, NT2, 128], BF16) for i in range(2)]
            actb = [fw.sb("actb%d" % i, [128, 512]) for i in range(2)]
            wgs = [fw.sb("wg%d" % i, [128, JG, TB2], BF16) for i in range(2)]
            nfin = fw.sb("nfin", [128, D])
            jk = fw.sb("jk", [128, D], BF16)
            ssq2 = fw.sb("ssq2", [128, NT2])
            rstd2 = fw.sb("rstd2", [128, NT2])
            fw.dma(SP, dnf, nfin[:], nfin_d.ap(), writes=[nfin])
            for sb2 in range((nsb * SBK) // TB2 if nsb * SBK >= TB2 else 1):
                tb0 = sb2 * TB2
                ntl = min(NT2, (nsb * SBK - tb0) // 128)
                ncol = ntl * 128
                for tl in range(ntl):
                    fw.dma(SP, dx2i[tl], acc[tl][:], x2_d.ap()[tb0 + tl * 128: tb0 + (tl + 1) * 128, :], reads=[x2Ts[tb0 // 128 + tl]], writes=[acc[tl]])
                if do_peer:
                    fw.dma(SP, dh2i, h2[:, :, 0:ncol], h2_d.ap()[:, :, tb0:tb0 + ncol], reads=[h2T_], writes=[h2])
                    nhalf = (ncol + 511) // 512
                    kpo = [0]

                    def stage1(jg, jj):
                        wg = wgs[jg % 2]
                        j = jg * JG + jj
                        u_ = ub[j % 2]
                        fw.dma(G, du[j % 2], u_[:], ut_d.ap()[j], writes=[u_])
                        g_ = gb[j % 2]
                        fw.dma(SP, dgi[j % 2], g_[:, 0:ntl, :], gd_d.ap()[j][:, tb0 // 128: tb0 // 128 + ntl, :], reads=[gdT], writes=[g_])
                        for tl in range(ntl):
                            fw.op(PE, lambda e, tl=tl: e.transpose(out=PG[:, tl * 128:(tl + 1) * 128], in_=g_[:, tl, :], identity=identb2[:]),
                                  reads=[g_, identb2], writes=[PG], inc=(tl == ntl - 1))
                        for hf in range(nhalf):
                            pa = PA[(j % 2) * 2 + hf]
                            for dcn in range(16):
                                fw.op(PE, lambda e, dcn=dcn: e.matmul(pa[:], lhsT=u_[:, dcn * 128:(dcn + 1) * 128], rhs=h2[:, dcn, hf * 512:(hf + 1) * 512],
                                                                      start=(dcn == 0), stop=(dcn == 15)), reads=[u_, h2], writes=[pa], inc=(dcn == 15))
                            ab = actb[hf]
                            fw.op(S, lambda e: e.activation(out=ab[:], in_=pa[:], func=AF.Gelu_apprx_tanh), reads=[pa], writes=[ab])
                            fw.op(V, lambda e: e.tensor_tensor(out=wg[:, jj, hf * 512:(hf + 1) * 512], in0=ab[:], in1=PG[:, hf * 512:(hf + 1) * 512], op=ALU.mult),
                                  reads=[ab, PG], writes=[wg])

                    def stage2(jg):
                        wg = wgs[jg % 2]
                        v_ = vb[jg % 2]
                        for tl in range(ntl):
                            for db in range(4):
                                po = PO[kpo[0] % 3]
                                kpo[0] += 1
                                for jj in range(JG):
                                    fw.op(PE, lambda e, jj=jj: e.matmul(po[:], lhsT=wg[:, jj, tl * 128:(tl + 1) * 128], rhs=v_[:, jj, db * 512:(db + 1) * 512],
                                                                        start=(jj == 0), stop=(jj == JG - 1)), reads=[wg, v_], writes=[po], inc=(jj == JG - 1))
                                fw.op(V, lambda e: e.tensor_tensor(out=acc[tl][:, db * 512:(db + 1) * 512], in0=po[:], in1=acc[tl][:, db * 512:(db + 1) * 512], op=ALU.add),
                                      reads=[po, acc[tl]], writes=[acc[tl]])

                    NJG = 128 // JG
                    for jg in range(NJG):
                        v_ = vb[jg % 2]
                        fw.dma(G, dv[jg % 2], v_[:], v_d.ap()[jg * JG:(jg + 1) * JG].rearrange("j p f -> p j f"), writes=[v_])
                        for jj in range(JG):
                            stage1(jg, jj)
                            if jj == 0 and jg > 0:
                                stage2(jg - 1)
                    stage2(NJG - 1)
                for tl in range(ntl):
                    x_t = acc[tl]
                    fw.op(S, lambda e: e.activation(out=jk[:], in_=x_t[:], func=AF.Square, accum_out=ssq2[:, tl:tl + 1]), reads=[x_t], writes=[jk, ssq2])
                    fw.op(V, lambda e: e.tensor_scalar(out=rstd2[:, tl:tl + 1], in0=ssq2[:, tl:tl + 1], scalar1=1.0 / D, scalar2=EPS, op0=ALU.mult, op1=ALU.add),
                          reads=[ssq2], writes=[rstd2])
                    fw.op(S, lambda e: e.activation(out=rstd2[:, tl:tl + 1], in_=rstd2[:, tl:tl + 1], func=AF.Sqrt), reads=[rstd2], writes=[rstd2])
                    fw.op(V, lambda e: e.reciprocal(out=rstd2[:, tl:tl + 1], in_=rstd2[:, tl:tl + 1]), reads=[rstd2], writes=[rstd2])
                    fw.op(V, lambda e: e.scalar_tensor_tensor(out=x_t[:], in0=x_t[:], scalar=rstd2[:, tl:tl + 1], in1=nfin[:], op0=ALU.mult, op1=ALU.mult),
                          reads=[x_t, rstd2, nfin], writes=[x_t])
                    fw.dma(SP, douts[tl], out_d.ap()[tb0 + tl * 128: tb0 + (tl + 1) * 128, :], x_t[:], reads=[x_t], writes=[outTs[tl]])
        waits = {}
        for t in outTs + taps:
            fw._need(SP, t.w, waits)
        for k, v in waits.items():
            nc.sync.wait_ge(fw.sems[k], v)
    return nc


def split_t(t, n):
    return [T("%s_%d" % (t.name, i), t.t) for i in range(n)]


def peer_route_gbuild(fw, pes, sbi, L):
    PF, PB, hT, mixT = L["PF"], L["PB"], L["hT"], L["mixT"]
    skt, identf, iota128 = L["skt"], L["identf"], L["iota128"]
    wq_d, gd_d, gdTs, dgo = L["wq_d"], L["gd_d"], L["gdTs"], L["dgo"]
    IT, JT, GT, dwq = L["IT"], L["JT"], L["GT"], L["dwq"]
    NJT = L["NJT"]
    tap = L["tap"]
    qT = mixT
    wqb = [fw.sb("wqb%d" % i, [128, 16, 128], BF16, es=pes) for i in range(2)]
    for g in range(16):
        wb = wqb[g % 2]
        fw.dma(G, dwq[g % 2], wb[:].rearrange("p a b -> p (a b)"), wq_d.ap()[g], writes=[wb])
        pf = PF[g % 2]
        for dcn in range(16):
            fw.op(PE, lambda e, dcn=dcn: e.matmul(pf[:], lhsT=wb[:, dcn, :], rhs=hT[:, dcn, :], start=(dcn == 0), stop=(dcn == 15)),
                  reads=[wb, hT], writes=[pf], inc=(dcn == 15))
        fw.op(S, lambda e: e.copy(out=qT[:, g, :], in_=pf[:]), reads=[pf], writes=[qT])
    stop_ = fw.sb("stop", [128, 16, 16], es=pes)
    itop = fw.sb("itop", [128, 16, 16], U32, es=pes)
    itopf = fw.sb("itopf", [128, 16, 16], es=pes)
    scr1 = [fw.sb("scr1_%d" % i, [128, 128], es=pes) for i in range(16)]
    scr2 = [fw.sb("scr2_%d" % i, [128, 256], es=pes) for i in range(8)]
    cand = fw.sb("cand", [128, 8, 256], es=pes)
    tv = fw.sb("tv", [128, 8, 16], es=pes)
    pos = fw.sb("pos", [128, 8, 16], U32, es=pes)
    posa = fw.sb("posa", [128, 8, 16], U32, es=pes)
    posf = fw.sb("posf", [128, 8, 16], es=pes)
    asel = fw.sb("asel", [128, 8, 16], es=pes)
    bsel = fw.sb("bsel", [128, 8, 16], es=pes)
    ohA = fw.sb("ohA", [128, 8, 16, 16], es=pes)
    ohB = fw.sb("ohB", [128, 8, 16, 16], es=pes)
    Itok = fw.sb("Itok", [128, 128], es=pes)
    Jtok = fw.sb("Jtok", [128, 128], es=pes)
    gtok = fw.sb("gtok", [128, 8, 16], es=pes)
    zsum = fw.sb("zsum", [128, 8], es=pes)
    stop_g = split_t(stop_, 16)
    itop_g = split_t(itop, 16)
    cand_h = split_t(cand, 8)
    tv_h = split_t(tv, 8)
    pos_h = split_t(pos, 8)
    PFq = [PF[g // 4] for g in range(16)]
    iota16 = iota128[:, 0:16]
    TB_ = 8
    NOB = 4
    ohI = [fw.sb("ohI%d" % i, [128, TB_, 128], BF16, es=pes) for i in range(NOB)]
    ohJ = [fw.sb("ohJ%d" % i, [128, TB_, 128], BF16, es=pes) for i in range(NOB)]
    ohIq = [split_t(o, TB_) for o in ohI]
    ohJq = [split_t(o, TB_) for o in ohJ]
    NGC = 4
    Gst = [fw.sb("Gst%d" % i, [128, 32, 128], BF16, es=pes) for i in range(NGC)]
    GPS = [(PF[5], PF[5][:]), (PB[0], PB[0][:].bitcast(F32)), (PB[1], PB[1][:].bitcast(F32))]
    cnt = {"ob": 0, "gc": 0, "gp": 0}

    def route_tile(tt):
        ts_ = slice(tt * 128, (tt + 1) * 128)
        for g in range(16):
            pq = PFq[g]
            yield fw.op(PE, lambda e, g=g: e.matmul(PF[g // 4][:, (g % 4) * 128:(g % 4 + 1) * 128], lhsT=qT[:, g, ts_], rhs=skt[:, g, :], start=True, stop=True),
                  reads=[qT, skt], writes=[pq], inc=(g % 4 == 3))
        sc = lambda g: PF[g // 4][:, (g % 4) * 128:(g % 4 + 1) * 128]
        for g in range(16):
            yield fw.op(V, lambda e, g=g: e.max(out=stop_[:, g, 0:8], in_=sc(g)), reads=[PFq[g]], writes=[stop_g[g]])
        for g in range(16):
            yield fw.op(V, lambda e, g=g: e.match_replace(out=scr1[g][:], in_to_replace=stop_[:, g, 0:8], in_values=sc(g), imm_value=NEG),
                  reads=[PFq[g], stop_g[g]], writes=[scr1[g]])
        for g in range(16):
            yield fw.op(V, lambda e, g=g: e.max_index(out=itop[:, g, 0:8], in_max=stop_[:, g, 0:8], in_values=sc(g)), reads=[PFq[g], stop_g[g]], writes=[itop_g[g]])
        for g in range(16):
            yield fw.op(V, lambda e, g=g: e.max(out=stop_[:, g, 8:16], in_=scr1[g][:]), reads=[scr1[g], stop_g[g]], writes=[stop_g[g]])
        for g in range(16):
            yield fw.op(V, lambda e, g=g: e.max_index(out=itop[:, g, 8:16], in_max=stop_[:, g, 8:16], in_values=sc(g)), reads=[PFq[g], stop_g[g], itop_g[g]], writes=[itop_g[g]])
        yield fw.op(V, lambda e: e.tensor_copy(itopf[:], itop[:]), reads=itop_g, writes=[itopf])
        for h in range(8):
            yield fw.op(V, lambda e, h=h: e.tensor_tensor(out=cand[:, h, :].rearrange("p (a b) -> p a b", b=16), in0=bc(stop_[:, 2 * h, :], 2, [128, 16, 16]),
                                                    in1=bc(stop_[:, 2 * h + 1, :], 1, [128, 16, 16]), op=ALU.add),
                  reads=[stop_g[2 * h], stop_g[2 * h + 1]], writes=[cand_h[h]])
        cd = lambda h: cand[:, h, :]
        for h in range(8):
            yield fw.op(V, lambda e, h=h: e.max(out=tv[:, h, 0:8], in_=cd(h)), reads=[cand_h[h]], writes=[tv_h[h]])
        for h in range(8):
            yield fw.op(V, lambda e, h=h: e.match_replace(out=scr2[h][:], in_to_replace=tv[:, h, 0:8], in_values=cd(h), imm_value=NEG),
                  reads=[cand_h[h], tv_h[h]], writes=[scr2[h]])
        for h in range(8):
            yield fw.op(V, lambda e, h=h: e.max_index(out=pos[:, h, 0:8], in_max=tv[:, h, 0:8], in_values=cd(h)), reads=[cand_h[h], tv_h[h]], writes=[pos_h[h]])
        for h in range(8):
            yield fw.op(V, lambda e, h=h: e.max(out=tv[:, h, 8:16], in_=scr2[h][:]), reads=[scr2[h], tv_h[h]], writes=[tv_h[h]])
        for h in range(8):
            yield fw.op(V, lambda e, h=h: e.max_index(out=pos[:, h, 8:16], in_max=tv[:, h, 8:16], in_values=cd(h)), reads=[cand_h[h], tv_h[h], pos_h[h]], writes=[pos_h[h]])
        yield fw.op(V, lambda e: e.tensor_copy(posf[:], pos[:]), reads=pos_h, writes=[posf])
        yield fw.op(V, lambda e: e.tensor_single_scalar(out=posa[:], in_=pos[:], scalar=4, op=ALU.logical_shift_right), reads=pos_h, writes=[posa])
        yield fw.op(V, lambda e: e.tensor_copy(asel[:], posa[:]), reads=[posa], writes=[asel])
        yield fw.op(V, lambda e: e.scalar_tensor_tensor(out=bsel[:], in0=asel[:], scalar=-16.0, in1=posf[:], op0=ALU.mult, op1=ALU.add),
              reads=[asel, posf], writes=[bsel])
        it4 = itopf[:].rearrange("p (h k) a -> p h k a", k=2)
        io4 = bc(bc(iota16, 1, [128, 16, 16]), 1, [128, 8, 16, 16])
        yield fw.op(V, lambda e: e.tensor_tensor(out=ohA[:], in0=bc(asel[:], 3, [128, 8, 16, 16]), in1=io4, op=ALU.is_equal), reads=[asel, iota128], writes=[ohA])
        yield fw.op(V, lambda e: e.tensor_tensor(out=ohB[:], in0=bc(bsel[:], 3, [128, 8, 16, 16]), in1=io4, op=ALU.is_equal), reads=[bsel, iota128], writes=[ohB])
        yield fw.op(V, lambda e: e.tensor_tensor(out=ohA[:], in0=ohA[:], in1=bc(it4[:, :, 0, :], 2, [128, 8, 16, 16]), op=ALU.mult), reads=[ohA, itopf], writes=[ohA])
        yield fw.op(V, lambda e: e.tensor_tensor(out=ohB[:], in0=ohB[:], in1=bc(it4[:, :, 1, :], 2, [128, 8, 16, 16]), op=ALU.mult), reads=[ohB, itopf], writes=[ohB])
        yield fw.op(V, lambda e: e.tensor_reduce(out=Itok[:].rearrange("p (h k) -> p h k", k=16), in_=ohA[:], axis=AX.X, op=ALU.add), reads=[ohA], writes=[Itok])
        yield fw.op(V, lambda e: e.tensor_reduce(out=Jtok[:].rearrange("p (h k) -> p h k", k=16), in_=ohB[:], axis=AX.X, op=ALU.add), reads=[ohB], writes=[Jtok])
        yield fw.op(V, lambda e: e.tensor_tensor(out=gtok[:], in0=tv[:], in1=bc(tv[:, :, 0], 2, [128, 8, 16]), op=ALU.subtract), reads=tv_h, writes=[gtok])
        yield fw.op(S, lambda e: e.activation(out=gtok[:], in_=gtok[:], func=AF.Exp), reads=[gtok], writes=[gtok])
        yield fw.op(V, lambda e: e.tensor_reduce(out=zsum[:], in_=gtok[:], axis=AX.X, op=ALU.add), reads=[gtok], writes=[zsum])
        yield fw.op(V, lambda e: e.reciprocal(out=zsum[:], in_=zsum[:]), reads=[zsum], writes=[zsum])
        yield fw.op(V, lambda e: e.tensor_scalar(out=zsum[:], in0=zsum[:], scalar1=1.0 / OH_HOT, scalar2=None, op0=ALU.mult), reads=[zsum], writes=[zsum])
        yield fw.op(V, lambda e: e.tensor_tensor(out=gtok[:], in0=gtok[:], in1=bc(zsum[:], 2, [128, 8, 16]), op=ALU.mult), reads=[gtok, zsum], writes=[gtok])
        for i, (src, srcap) in enumerate(((Itok, Itok[:]), (Jtok, Jtok[:]), (gtok, gtok[:].rearrange("p a b -> p (a b)")))):
            yield fw.op(PE, lambda e, i=i, srcap=srcap: e.transpose(out=PF[4][:, i * 128:(i + 1) * 128], in_=srcap, identity=identf[:]),
                  reads=[src, identf], writes=[PF[4]], inc=(i == 2))
        yield fw.op(S, lambda e: e.copy(out=IT[:, ts_], in_=PF[4][:, 0:128]), reads=[PF[4]], writes=[IT])
        yield fw.op(S, lambda e: e.copy(out=JT[:, ts_], in_=PF[4][:, 128:256]), reads=[PF[4]], writes=[JT])
        yield fw.op(S, lambda e: e.activation(out=NJT[:, ts_], in_=PF[4][:, 128:256], func=AF.Copy, scale=-OHS), reads=[PF[4]], writes=[NJT])
        yield fw.op(S, lambda e: e.copy(out=GT[:, ts_], in_=PF[4][:, 256:384]), reads=[PF[4]], writes=[GT])

    def gbuild_tile(tt, nxt=None, kstep=11):
        tile_g = sbi * NTT + tt
        for tb in range(128 // TB_):
            oi = ohI[cnt["ob"] % NOB]
            oiq = ohIq[cnt["ob"] % NOB]
            oj = ohJ[cnt["ob"] % NOB]
            ojq = ohJq[cnt["ob"] % NOB]
            cnt["ob"] += 1
            cs0 = tt * 128 + tb * TB_
            fw.op(V, lambda e: e.tensor_tensor(out=oi[:], in0=bc(iota128[:], 1, [128, TB_, 128]), in1=bc(IT[:, cs0:cs0 + TB_], 2, [128, TB_, 128]),
                                               op=ALU.is_equal), reads=[iota128, IT], writes=oiq)
            for q in range(TB_):
                fw.op(S, lambda e, q=q: e.activation(out=oj[:, q, :], in_=iota128[:], func=AF.Derivative_Erf, scale=OHS, bias=NJT[:, cs0 + q:cs0 + q + 1]),
                      reads=[iota128, NJT], writes=[ojq[q]])
            fw.op(G, lambda e: e.tensor_tensor(out=oi[:], in0=oi[:], in1=bc(GT[:, cs0:cs0 + TB_], 2, [128, TB_, 128]), op=ALU.mult),
                  reads=oiq + [GT], writes=oiq)
            if nxt is not None:
                for _ in range(kstep):
                    next(nxt, None)
            for q in range(TB_):
                t = tb * TB_ + q
                if t % 4 == 0:
                    gpT, gpap = GPS[cnt["gp"] % 3]
                    cnt["gp"] += 1
                fw.op(PE, lambda e, t=t, q=q, gpap=gpap: e.matmul(gpap[:, (t % 4) * 128:(t % 4 + 1) * 128], lhsT=oj[:, q, :], rhs=oi[:, q, :], start=True, stop=True),
                      reads=[oiq[q], ojq[q]], writes=[gpT], inc=(t % 4 == 3))
                if t % 32 == 0:
                    gci = cnt["gc"] % NGC
                    gs = Gst[gci]
                    cnt["gc"] += 1
                if t % 4 == 3:
                    tq = (t % 32) - 3
                    fw.op(S, lambda e, gpap=gpap, tq=tq, gs=gs: e.copy(out=gs[:, tq:tq + 4, :].rearrange("p t i -> p (t i)"), in_=gpap),
                          reads=[gpT], writes=[gs])
                if t % 32 == 31:
                    tl0 = t - 31
                    for jh in range(2):
                        fw.dma(SP, dgo[gci], gd_d.ap()[jh * 64:(jh + 1) * 64, tl0:tl0 + 32, tile_g, :], gs[jh * 64:(jh + 1) * 64, :, :], reads=[gs], writes=[gdTs[gci]])

    for _ in route_tile(0):
        pass
    for tt in range(NTT):
        nxt = route_tile(tt + 1) if tt + 1 < NTT else None
        gbuild_tile(tt, nxt)
        if nxt is not None:
            for _ in nxt:
                pass
    if sbi == 0:
        tap(IT, IT[:, 0:128], 128)
        tap(JT, JT[:, 0:128], 128)
        tap(GT, GT[:, 0:128], 128)


def _lay_k(w, ncols_blk):
    C = w.shape[1]
    nb = C // ncols_blk
    a = w.reshape(16, 128, nb, ncols_blk).transpose(2, 1, 0, 3)
    return np.ascontiguousarray(a).reshape(nb, 128, 16 * ncols_blk)


def prepare_inputs(inp):
    f = lambda a: np.asarray(a, dtype=np.float32)
    w_in = f(inp["w_in"])[0]
    w128 = np.concatenate([w_in[:, 0:2048], w_in[:, 3072:4608]], axis=1)
    win = _lay_k(w128, 128)
    wz = _lay_k(w_in[:, 2048:3072], 512)
    wdt = _lay_k(w_in[:, 4608:4624], 16)[0]
    wout = _lay_k(f(inp["w_out"])[0], 512)
    wq = _lay_k(f(inp["peer_wq"])[0], 128)
    sp = np.zeros((128, NSP), np.float32)
    col = lambda v, n: f(v).reshape(n, 128).T
    sp[:, O_NMW:O_NMW + 16] = col(inp["norm_mix_w"][0], 16)
    sp[:, O_NFW:O_NFW + 16] = col(inp["norm_ffn_w"][0], 16)
    lcw = f(inp["lru_conv_w"])[0]
    sp[:, O_LCW:O_LCW + 32] = lcw.reshape(4, 8, 128).transpose(2, 1, 0).reshape(128, 32)
    sp[:, O_LCB:O_LCB + 8] = col(inp["lru_conv_b"][0], 8)
    sp[:, O_LBA:O_LBA + 8] = col(inp["lru_ba"][0], 8)
    sp[:, O_LBX:O_LBX + 8] = col(inp["lru_bx"][0], 8)
    sp[:, O_LAM:O_LAM + 8] = col(inp["lru_lambda"][0], 8)
    scw = f(inp["ssd_conv_w"])[0]
    sp[:, O_SCW:O_SCW + 48] = scw.reshape(4, 12, 128).transpose(2, 1, 0).reshape(128, 48)
    sp[:, O_SCB:O_SCB + 12] = col(inp["ssd_conv_b"][0], 12)
    sp[:, O_SNW:O_SNW + 8] = col(inp["ssd_norm_w"][0], 8)
    sp[:, O_DTB:O_DTB + 16] = np.broadcast_to(f(inp["ssd_dt_bias"])[0][None, :], (128, 16))
    sp[:, O_ALOG:O_ALOG + 16] = np.broadcast_to(f(inp["ssd_a_log"])[0][None, :], (128, 16))
    sp[:, O_SD:O_SD + 16] = np.broadcast_to(f(inp["ssd_d"])[0][None, :], (128, 16))
    nfin = np.ascontiguousarray(np.broadcast_to(f(inp["norm_final_w"])[None, :], (128, D)))

    def bd(w):
        o = np.zeros((128, 8, 128), np.float32)
        for k in range(8):
            o[0:64, k, 0:64] = w[2 * k]
            o[64:128, k, 64:128] = w[2 * k + 1]
        return o.reshape(128, 8 * 128)
    wabd = bd(f(inp["lru_wa"])[0])
    wxbd = bd(f(inp["lru_wx"])[0])
    sk = f(inp["peer_sub_keys"])[0]
    skt = np.ascontiguousarray(sk.reshape(16, 128, 128).transpose(2, 0, 1)).reshape(128, 16 * 128)
    u = f(inp["peer_u"])[0]
    ut = np.ascontiguousarray(u.reshape(128, 128, 16, 128).transpose(1, 3, 2, 0)).reshape(128, 128, 16 * 128)
    vv = f(inp["peer_v"])[0]
    vl = np.ascontiguousarray(vv.reshape(128, 128, D).transpose(1, 0, 2))
    shared = dict(sp=sp, nfin=nfin, win=win, wz=wz, wdt=wdt, wabd=wabd, wxbd=wxbd, wout=wout, wq=wq, skt=skt, ut=ut, v=vl)
    return shared


_NC_CACHE = {}


def kernel(**inputs):
    x = np.asarray(inputs["x"], dtype=np.float32)
    shared = prepare_inputs(inputs)
    if "nc" not in _NC_CACHE:
        _NC_CACHE["nc"] = build_nc()
    nc = _NC_CACHE["nc"]
    in_maps = [dict(shared, x=np.ascontiguousarray(x[c])) for c in range(8)]
    res = run_bass_kernel_spmd(nc, in_maps, core_ids=list(range(8)))
    return np.stack([np.asarray(r["out"], dtype=np.float32) for r in res.results], axis=0)
```
